# Optimizing a Trainium2 kernel written in Bass

```python
import math
import jax, jax.numpy as jnp
from jax import lax
import numpy as np

D_MODEL = 1024
BATCH = 8
SEQ = 4096
DEPTH = 2
DEC_BATCH = 8
DEC_SEQ = 2048
PAST_LEN = 128

N_MEM = 256
D_FF = 4 * D_MODEL
CROSS_HEADS = 4
CROSS_HEAD_DIM = D_MODEL // CROSS_HEADS
S5_WIDTH = D_MODEL // 2
S5_GROUP = 16
S5_GROUPS = S5_WIDTH // S5_GROUP
S5_STATE = 64
SGU_WIDTH = D_MODEL // 2
SGU_HEADS = 4
SGU_HEAD_DIM = SGU_WIDTH // SGU_HEADS
SGU_CHUNK = 128
AB_IN = S5_WIDTH + 2 * SGU_WIDTH
AB_OUT = S5_WIDTH + SGU_WIDTH
POOL_WINDOWS = (2, 4, 8, 16)
POOL_GROUPS = len(POOL_WINDOWS)
POOL_GROUP_DIM = D_MODEL // POOL_GROUPS
N_EVEN = (DEPTH + 1) // 2
N_ODD = DEPTH // 2
EPS = 1e-6

kernel_name = "hybrid_s5_sgu_pool_macaron_encoder"


def _rmsnorm(x, g):
    xf = x.astype(jnp.float32)
    y = xf * lax.rsqrt(jnp.mean(xf * xf, axis=-1, keepdims=True) + EPS)
    return (y * g.astype(jnp.float32)).astype(x.dtype)


def _layernorm(x, g, b):
    xf = x.astype(jnp.float32)
    mu = jnp.mean(xf, axis=-1, keepdims=True)
    xc = xf - mu
    var = jnp.mean(xc * xc, axis=-1, keepdims=True)
    y = xc * lax.rsqrt(var + EPS) * g.astype(jnp.float32) + b.astype(jnp.float32)
    return y.astype(x.dtype)


def _swiglu(h, w_gate, w_up, w_down):
    return (jax.nn.silu(h @ w_gate) * (h @ w_up)) @ w_down


def _complex_linear_combine(left, right):
    a1r, a1i, b1r, b1i = left
    a2r, a2i, b2r, b2i = right
    ar = a2r * a1r - a2i * a1i
    ai = a2r * a1i + a2i * a1r
    br = a2r * b1r - a2i * b1i + b2r
    bi = a2r * b1i + a2i * b1r + b2i
    return (ar, ai, br, bi)


def _s5_direction(uf, lam_re, lam_im, log_dt, b_re, b_im, c_re, c_im, reverse):
    L = uf.shape[1]
    dt = jnp.exp(log_dt)[:, None]
    mag = jnp.exp(lam_re * dt)
    ang = lam_im * dt
    ab_re = mag * jnp.cos(ang)
    ab_im = mag * jnp.sin(ang)
    nr = ab_re - 1.0
    ni = ab_im
    den = lam_re * lam_re + lam_im * lam_im
    q_re = (nr * lam_re + ni * lam_im) / den
    q_im = (ni * lam_re - nr * lam_im) / den
    bb_re = q_re[..., None] * b_re - q_im[..., None] * b_im
    bb_im = q_re[..., None] * b_im + q_im[..., None] * b_re
    x_re = jnp.einsum('blgc,gpc->blgp', uf, bb_re)
    x_im = jnp.einsum('blgc,gpc->blgp', uf, bb_im)
    a_shape = (1, L) + ab_re.shape
    a_re = jnp.broadcast_to(ab_re, a_shape)
    a_im = jnp.broadcast_to(ab_im, a_shape)
    _, _, h_re, h_im = lax.associative_scan(
        _complex_linear_combine, (a_re, a_im, x_re, x_im), reverse=reverse, axis=1)
    return (jnp.einsum('blgp,gcp->blgc', h_re, c_re)
            - jnp.einsum('blgp,gcp->blgc', h_im, c_im))


def _s5_mixer(u, lam_re, lam_im, log_dt, b_re, b_im, c_re, c_im, d, w_glu):
    f32 = jnp.float32
    Bn, L, W = u.shape
    uf = u.astype(f32)
    ug = uf.reshape(Bn, L, S5_GROUPS, S5_GROUP)
    y_fwd = _s5_direction(ug, lam_re[0].astype(f32), lam_im[0].astype(f32), log_dt[0].astype(f32),
                          b_re[0].astype(f32), b_im[0].astype(f32),
                          c_re[0].astype(f32), c_im[0].astype(f32), False)
    y_bwd = _s5_direction(ug, lam_re[1].astype(f32), lam_im[1].astype(f32), log_dt[1].astype(f32),
                          b_re[1].astype(f32), b_im[1].astype(f32),
                          c_re[1].astype(f32), c_im[1].astype(f32), True)
    y = (y_fwd + y_bwd).reshape(Bn, L, W) + d.astype(f32) * uf
    g = jax.nn.gelu(y).astype(u.dtype)
    return g * jax.nn.sigmoid(g @ w_glu)


def _sgu_mixer(u, v, norm_g, norm_b, w_s, b_s):
    Bn, L, W = u.shape
    u = jax.nn.gelu(u)
    v = _layernorm(jax.nn.gelu(v), norm_g, norm_b)
    vc = v.reshape(Bn, L // SGU_CHUNK, SGU_CHUNK, SGU_HEADS, SGU_HEAD_DIM)
    vs = jnp.einsum('bnjhd,hij->bnihd', vc, w_s) + b_s.T[:, :, None]
    return u * vs.reshape(Bn, L, W)


def _pool_mixer(h, pool_w, pool_scale):
    f32 = jnp.float32
    Bn, L, D = h.shape
    hf = h.astype(f32)
    cs = jnp.concatenate([jnp.zeros((Bn, 1, D), f32), jnp.cumsum(hf, axis=1)], axis=1)
    t = jnp.arange(L)
    outs = []
    for gi, w in enumerate(POOL_WINDOWS):
        lo = jnp.clip(t - w // 2, 0, L)
        hi = jnp.clip(t + w // 2, 0, L)
        csg = cs[..., gi * POOL_GROUP_DIM:(gi + 1) * POOL_GROUP_DIM]
        s = jnp.take(csg, hi, axis=1) - jnp.take(csg, lo, axis=1)
        cnt = (hi - lo).astype(f32)[None, :, None]
        pooled = s / cnt - hf[..., gi * POOL_GROUP_DIM:(gi + 1) * POOL_GROUP_DIM]
        outs.append(jnp.einsum('bld,de->ble', pooled, pool_w[gi].astype(f32)))
    y = jnp.concatenate(outs, axis=-1) * pool_scale.astype(f32)
    return y.astype(h.dtype)


def _cross_attn(h, m, w_q, w_kv, w_o):
    Bn, L, D = h.shape
    M = m.shape[1]
    q = (h @ w_q).reshape(Bn, L, CROSS_HEADS, CROSS_HEAD_DIM)
    kv = m @ w_kv
    k = kv[..., :D].reshape(Bn, M, CROSS_HEADS, CROSS_HEAD_DIM)
    v = kv[..., D:].reshape(Bn, M, CROSS_HEADS, CROSS_HEAD_DIM)
    s = jnp.einsum('blhd,bmhd->bhlm', q, k).astype(jnp.float32) * (CROSS_HEAD_DIM ** -0.5)
    p = jax.nn.softmax(s, axis=-1).astype(h.dtype)
    o = jnp.einsum('bhlm,bmhd->blhd', p, v).reshape(Bn, L, D)
    return o @ w_o


def _trunk(x, mem, p):
    for i in range(DEPTH):
        h = _rmsnorm(x, p['ffn1_norm'][i])
        x = x + 0.5 * _swiglu(h, p['ffn1_w_gate'][i], p['ffn1_w_up'][i], p['ffn1_w_down'][i])
        h = _rmsnorm(x, p['mix_norm'][i])
        if i % 2 == 0:
            e = i // 2
            z = h @ p['ab_w_in'][e]
            ua = z[..., :S5_WIDTH]
            ub = z[..., S5_WIDTH:S5_WIDTH + SGU_WIDTH]
            vb = z[..., S5_WIDTH + SGU_WIDTH:]
            ya = _s5_mixer(ua, p['s5_lambda_re'][e], p['s5_lambda_im'][e], p['s5_log_dt'][e],
                           p['s5_b_re'][e], p['s5_b_im'][e], p['s5_c_re'][e], p['s5_c_im'][e],
                           p['s5_d'][e], p['s5_w_glu'][e])
            yb = _sgu_mixer(ub, vb, p['sgu_norm_g'][e], p['sgu_norm_b'][e],
                            p['sgu_w_s'][e], p['sgu_b_s'][e])
            x = x + jnp.concatenate([ya, yb], axis=-1) @ p['ab_w_out'][e]
        else:
            o = i // 2
            x = x + _pool_mixer(h, p['pool_w'][o], p['pool_scale'][o])
        hm = _rmsnorm(mem, p['mem_norm'][i])
        h = _rmsnorm(x, p['cross_norm'][i])
        x = x + _cross_attn(h, hm, p['cross_w_q'][i], p['cross_w_kv'][i], p['cross_w_o'][i])
        h = _rmsnorm(x, p['ffn2_norm'][i])
        x = x + 0.5 * _swiglu(h, p['ffn2_w_gate'][i], p['ffn2_w_up'][i], p['ffn2_w_down'][i])
    return _rmsnorm(x, p['final_norm'])


def setup_inputs(seed: int = 0) -> dict:
    key = jax.random.key(seed)
    keys = iter(jax.random.split(key, 48))
    f32 = jnp.float32

    def nrm(shape, scale):
        return jax.random.normal(next(keys), shape, f32) * scale

    def gain(shape):
        return 1.0 + 0.01 * jax.random.normal(next(keys), shape, f32)

    D, F, G, P, C = D_MODEL, D_FF, S5_GROUPS, S5_STATE, S5_GROUP
    inp = {}
    inp['x_prompt'] = nrm((BATCH, SEQ, D), 1.0)
    inp['x_sample'] = nrm((DEC_BATCH, DEC_SEQ, D), 1.0)
    inp['mem_prompt'] = nrm((BATCH, N_MEM, D), 1.0)
    inp['mem_sample'] = nrm((DEC_BATCH, N_MEM, D), 1.0)
    inp['ffn1_norm'] = gain((DEPTH, D))
    inp['ffn1_w_gate'] = nrm((DEPTH, D, F), D ** -0.5)
    inp['ffn1_w_up'] = nrm((DEPTH, D, F), D ** -0.5)
    inp['ffn1_w_down'] = nrm((DEPTH, F, D), F ** -0.5)
    inp['mix_norm'] = gain((DEPTH, D))
    inp['ab_w_in'] = nrm((N_EVEN, D, AB_IN), D ** -0.5)
    inp['s5_lambda_re'] = -0.5 + 0.01 * jax.random.normal(next(keys), (N_EVEN, 2, G, P), f32)
    inp['s5_lambda_im'] = (np.pi * jnp.arange(P, dtype=f32)
                           + 0.01 * jax.random.normal(next(keys), (N_EVEN, 2, G, P), f32))
    inp['s5_log_dt'] = jax.random.uniform(next(keys), (N_EVEN, 2, G), f32,
                                          minval=math.log(1e-3), maxval=math.log(1e-1))
    inp['s5_b_re'] = nrm((N_EVEN, 2, G, P, C), (2 * C) ** -0.5)
    inp['s5_b_im'] = nrm((N_EVEN, 2, G, P, C), (2 * C) ** -0.5)
    inp['s5_c_re'] = nrm((N_EVEN, 2, G, C, P), P ** -0.5)
    inp['s5_c_im'] = nrm((N_EVEN, 2, G, C, P), P ** -0.5)
    inp['s5_d'] = nrm((N_EVEN, S5_WIDTH), 1.0)
    inp['s5_w_glu'] = nrm((N_EVEN, S5_WIDTH, S5_WIDTH), S5_WIDTH ** -0.5)
    inp['sgu_norm_g'] = gain((N_EVEN, SGU_WIDTH))
    inp['sgu_norm_b'] = nrm((N_EVEN, SGU_WIDTH), 0.01)
    inp['sgu_w_s'] = nrm((N_EVEN, SGU_HEADS, SGU_CHUNK, SGU_CHUNK), SGU_CHUNK ** -0.5)
    inp['sgu_b_s'] = gain((N_EVEN, SGU_HEADS, SGU_CHUNK))
    inp['ab_w_out'] = nrm((N_EVEN, AB_OUT, D), AB_OUT ** -0.5)
    inp['pool_w'] = nrm((N_ODD, POOL_GROUPS, POOL_GROUP_DIM, POOL_GROUP_DIM), POOL_GROUP_DIM ** -0.5)
    inp['pool_scale'] = gain((N_ODD, D))
    inp['cross_norm'] = gain((DEPTH, D))
    inp['mem_norm'] = gain((DEPTH, D))
    inp['cross_w_q'] = nrm((DEPTH, D, D), D ** -0.5)
    inp['cross_w_kv'] = nrm((DEPTH, D, 2 * D), D ** -0.5)
    inp['cross_w_o'] = nrm((DEPTH, D, D), D ** -0.5)
    inp['ffn2_norm'] = gain((DEPTH, D))
    inp['ffn2_w_gate'] = nrm((DEPTH, D, F), D ** -0.5)
    inp['ffn2_w_up'] = nrm((DEPTH, D, F), D ** -0.5)
    inp['ffn2_w_down'] = nrm((DEPTH, F, D), F ** -0.5)
    inp['final_norm'] = gain((D,))
    return inp


def reference(x_prompt, x_sample, mem_prompt, mem_sample,
              ffn1_norm, ffn1_w_gate, ffn1_w_up, ffn1_w_down,
              mix_norm, ab_w_in,
              s5_lambda_re, s5_lambda_im, s5_log_dt, s5_b_re, s5_b_im, s5_c_re, s5_c_im,
              s5_d, s5_w_glu,
              sgu_norm_g, sgu_norm_b, sgu_w_s, sgu_b_s, ab_w_out,
              pool_w, pool_scale,
              cross_norm, mem_norm, cross_w_q, cross_w_kv, cross_w_o,
              ffn2_norm, ffn2_w_gate, ffn2_w_up, ffn2_w_down,
              final_norm):
    p = dict(ffn1_norm=ffn1_norm, ffn1_w_gate=ffn1_w_gate, ffn1_w_up=ffn1_w_up,
             ffn1_w_down=ffn1_w_down, mix_norm=mix_norm, ab_w_in=ab_w_in,
             s5_lambda_re=s5_lambda_re, s5_lambda_im=s5_lambda_im, s5_log_dt=s5_log_dt,
             s5_b_re=s5_b_re, s5_b_im=s5_b_im, s5_c_re=s5_c_re, s5_c_im=s5_c_im,
             s5_d=s5_d, s5_w_glu=s5_w_glu,
             sgu_norm_g=sgu_norm_g, sgu_norm_b=sgu_norm_b, sgu_w_s=sgu_w_s, sgu_b_s=sgu_b_s,
             ab_w_out=ab_w_out, pool_w=pool_w, pool_scale=pool_scale,
             cross_norm=cross_norm, mem_norm=mem_norm, cross_w_q=cross_w_q,
             cross_w_kv=cross_w_kv, cross_w_o=cross_w_o,
             ffn2_norm=ffn2_norm, ffn2_w_gate=ffn2_w_gate, ffn2_w_up=ffn2_w_up,
             ffn2_w_down=ffn2_w_down, final_norm=final_norm)
    y_prompt = _trunk(x_prompt, mem_prompt, p)
    y_sample = _trunk(x_sample, mem_sample, p)
    return (y_prompt, y_sample)
```

```python
import math
import numpy as np
import concourse.bass as bass
import concourse.mybir as mybir
from concourse.bass_utils import run_bass_kernel_spmd

F32 = mybir.dt.float32
BF16 = mybir.dt.bfloat16
I32 = mybir.dt.int32
ALU = mybir.AluOpType
AF = mybir.ActivationFunctionType
AX = mybir.AxisListType

D = 1024
DT = 8
DFF = 4096
LP = 4096
LS = 2048
LT = LP + LS
NMEM = 256
NT = 1024
HALO = 8
XW = NT + 2 * HALO
EPS = 1e-6
RING = 16
HOLD = 8

OFF_X = 0
OFF_H = OFF_X + DT * XW * 4
OFF_ACT = OFF_H + DT * XW * 2
OFF_RING = OFF_ACT + 32 * NT * 2
ARENA = OFF_RING + RING * 1024 * 2

TB = 16
NB = LT // TB
NBP = LP // TB
NJ = NT // TB
S5C = dict(LR=0, LI=32, LDT=64, BR=96, BI=608, CR=1120, CI=1632, DL=2144, N=2176)

STAGE = 99


class Buf:
    __slots__ = ("w", "r")

    def __init__(self):
        self.w = None
        self.r = {}


class Eng:
    def __init__(self, name, sem, si):
        self.name = name
        self.sem = sem
        self.si = si
        self.n = 0
        self.prog = []
        self.know = None


class DSem:
    def __init__(self, sem, si):
        self.sem = sem
        self.si = si
        self.count = 0


class KB:
    NS = 64

    def __init__(self, nc, sems):
        self.nc = nc
        self.sems = list(sems)
        self.semlist = []
        self.E = {}
        for nm in ("pe", "act", "dve", "pool", "sp"):
            self.E[nm] = Eng(nm, *self._newsem())
            self.E[nm].know = np.zeros(self.NS, np.int64)
        self.dsems = []
        self.gen = [self.new_dsem() for _ in range(16)]
        self.geni = 0
        self.snap = {}
        self.seq = 0

    def _newsem(self):
        s = self.sems.pop()
        self.semlist.append(s)
        return s, len(self.semlist) - 1

    def new_dsem(self):
        d = DSem(*self._newsem())
        self.dsems.append(d)
        return d

    def _waits(self, eng, reads, writes, extra=()):
        need = {}

        def add(tok):
            if tok is None:
                return
            si, v = tok
            if need.get(si, 0) < v:
                need[si] = v

        for b in reads:
            add(b.w)
        for b in writes:
            add(b.w)
            for t in b.r.items():
                add(t)
        for t in extra:
            add(t)
        toks = sorted(need.items(), key=lambda t: -self.snap[t][0])
        for (si, v) in toks:
            if eng.know[si] >= v:
                continue
            eng.prog.append(("w", si, v))
            np.maximum(eng.know, self.snap[(si, v)][1], out=eng.know)

    def _commit(self, eng, tok, reads, writes):
        si, v = tok
        self.seq += 1
        k = eng.know.copy()
        k[si] = max(k[si], v)
        self.snap[tok] = (self.seq, k)
        for b in reads:
            if b.r.get(si, 0) < v:
                b.r[si] = v
        for b in writes:
            b.w = tok
            b.r = {}

    def op(self, en, reads, writes, meth, **kw):
        eng = self.E[en]
        self._waits(eng, reads, writes)
        eng.n += 1
        eng.prog.append(("i", meth, kw, eng.si, eng.n))
        self._commit(eng, (eng.si, eng.n), reads, writes)

    def group(self, en, reads, writes, calls):
        eng = self.E[en]
        self._waits(eng, reads, writes)
        for (m, kw) in calls[:-1]:
            eng.prog.append(("i", m, kw, None, 0))
        eng.n += 1
        eng.prog.append(("i", calls[-1][0], calls[-1][1], eng.si, eng.n))
        self._commit(eng, (eng.si, eng.n), reads, writes)

    def dma(self, reads, writes, out, in_, dsem=None, q="sp", **kw):
        eng = self.E[q]
        if dsem is None:
            dsem = self.gen[self.geni % len(self.gen)]
            self.geni += 1
        extra = [(dsem.si, dsem.count)] if dsem.count else []
        self._waits(eng, reads, writes, extra)
        dsem.count += 16
        eng.prog.append(("i", "dma_start", dict(out=out, in_=in_, **kw), dsem.si, dsem.count))
        self._commit(eng, (dsem.si, dsem.count), reads, writes)

    def barrier(self):
        for eng in self.E.values():
            for o in self.E.values():
                if o.n and eng.know[o.si] < o.n:
                    eng.prog.append(("w", o.si, o.n))
                    np.maximum(eng.know, self.snap[(o.si, o.n)][1], out=eng.know)
            for d in self.dsems:
                if d.count and eng.know[d.si] < d.count:
                    eng.prog.append(("w", d.si, d.count))
                    np.maximum(eng.know, self.snap[(d.si, d.count)][1], out=eng.know)

    def emit(self):
        nc = self.nc
        hmap = {"pe": "tensor", "act": "scalar", "dve": "vector", "pool": "gpsimd", "sp": "sync"}
        waited = set()
        for eng in self.E.values():
            for it in eng.prog:
                if it[0] == "w":
                    waited.add((it[1], it[2]))
        remap = {}
        comp_si = {eng.si for eng in self.E.values()}
        for eng in self.E.values():
            rank = 0
            for it in eng.prog:
                if it[0] == "i" and it[3] == eng.si:
                    if (eng.si, it[4]) in waited:
                        rank += 1
                        remap[(eng.si, it[4])] = rank
        sl = self.semlist

        def replay(h, prog, attach=True):
            pend = []
            for it in prog:
                if it[0] == "w":
                    si, v = it[1], it[2]
                    if si in comp_si:
                        v = remap[(si, v)]
                    pend.append((sl[si], v))
                    continue
                if it[1] == "dma_start" or not attach:
                    for (s, v) in pend:
                        h.wait_ge(s, v)
                    pend = []
                for (s, v) in pend[:-1]:
                    h.wait_ge(s, v)
                ins = getattr(h, it[1])(**it[2])
                if pend:
                    ins._wait_ge(pend[-1][0], pend[-1][1])
                pend = []
                if it[3] is not None:
                    if it[3] in comp_si:
                        if (it[3], it[4]) in remap:
                            ins.then_inc(sl[it[3]], 1)
                    else:
                        ins.then_inc(sl[it[3]], 16)
            for (s, v) in pend:
                h.wait_ge(s, v)

        with nc.Block() as block:
            for nm, eng in self.E.items():
                getattr(block, hmap[nm])(lambda h, p=eng.prog: replay(h, p))


def units_lhsT(W):
    K_, M_ = W.shape
    a = W.reshape(K_ // 1024, 8, 128, M_ // 128, 128).transpose(3, 0, 2, 1, 4)
    return np.ascontiguousarray(a).reshape(-1, 128, 1024)


def units_rhs(W):
    K_, N_ = W.shape
    a = W.reshape(4, 2, 128, N_ // 512, 512).transpose(3, 0, 2, 1, 4)
    return np.ascontiguousarray(a).reshape(-1, 128, 1024)


def units_small(W, nk, nm):
    a = W.reshape(nk, 128, nm, 128).transpose(2, 0, 1, 3)
    a = a.reshape(nm * nk, 128, 128)
    nu = (nm * nk) // 8
    a = a.reshape(nu, 8, 128, 128).transpose(0, 2, 1, 3)
    return np.ascontiguousarray(a).reshape(nu, 128, 1024)


def pack_pool(pw):
    a = pw.reshape(2, 2, 2, 128, 2, 128)
    a = a.transpose(0, 3, 1, 4, 2, 5)
    return np.ascontiguousarray(a).reshape(2, 128, 1024)


def pack_weights(inp):
    parts, table, base = [], {}, 0

    def add(name, arr):
        nonlocal base
        table[name] = base
        parts.append(arr)
        base += arr.shape[0]

    for i in range(2):
        for f in ("ffn1", "ffn2"):
            add(f"{f}_gate{i}", units_lhsT(inp[f + "_w_gate"][i]))
            add(f"{f}_up{i}", units_lhsT(inp[f + "_w_up"][i]))
            add(f"{f}_down{i}", units_lhsT(inp[f + "_w_down"][i]))
        add(f"wq{i}", units_lhsT(inp["cross_w_q"][i]))
        add(f"wk{i}", units_lhsT(inp["cross_w_kv"][i][:, :D]))
        add(f"wv{i}", units_rhs(inp["cross_w_kv"][i][:, D:]))
        add(f"wo{i}", units_lhsT(inp["cross_w_o"][i]))
    add("win_a", units_lhsT(inp["ab_w_in"][0][:, :1024]))
    add("win_v", units_rhs(inp["ab_w_in"][0][:, 1024:]))
    add("wglu", units_small(inp["s5_w_glu"][0], 4, 4))
    add("wout", units_lhsT(inp["ab_w_out"][0]))
    add("wpool", pack_pool(inp["pool_w"][0]))
    return np.concatenate(parts, 0), table


VEC_NAMES = ["ffn1_norm0", "ffn1_norm1", "mix_norm0", "mix_norm1", "cross_norm0", "cross_norm1",
             "mem_norm0", "mem_norm1", "ffn2_norm0", "ffn2_norm1", "final_norm", "pool_scale"]


def pack_vecs(inp):
    vs = []
    for nm in VEC_NAMES:
        if nm == "final_norm":
            v = inp["final_norm"]
        elif nm == "pool_scale":
            v = inp["pool_scale"][0]
        else:
            v = inp[nm[:-1]][int(nm[-1])]
        vs.append(np.asarray(v, np.float32).reshape(DT, 128).T)
    return np.ascontiguousarray(np.concatenate(vs, 1))


def pack_s5(inp):
    f = np.float32

    def lay_gn(a):
        return np.asarray(a, f).reshape(2, 16, 2, 64).transpose(2, 3, 0, 1).reshape(128, 32)

    def lay_b(a):
        return np.asarray(a, f).reshape(2, 16, 2, 64, 16).transpose(2, 3, 0, 1, 4).reshape(128, 512)

    def lay_c(a):
        return np.asarray(a, f).reshape(2, 16, 2, 16, 64).transpose(2, 4, 0, 1, 3).reshape(128, 512)

    ldt = np.broadcast_to(np.asarray(inp["s5_log_dt"][0], f)[:, :, None], (2, 32, 64))
    dl = np.broadcast_to(np.asarray(inp["s5_d"][0], f).reshape(1, 32, 16), (8, 32, 16)).transpose(0, 2, 1).reshape(128, 32)
    parts = [lay_gn(inp["s5_lambda_re"][0]), lay_gn(inp["s5_lambda_im"][0]), lay_gn(ldt),
             lay_b(inp["s5_b_re"][0]), lay_b(inp["s5_b_im"][0]), lay_c(inp["s5_c_re"][0]), lay_c(inp["s5_c_im"][0]), dl]
    return np.ascontiguousarray(np.concatenate(parts, 1))


def pack_sgu(inp):
    f = np.float32
    ws = np.asarray(inp["sgu_w_s"][0], f).transpose(2, 0, 1).reshape(128, 512)
    g = np.broadcast_to(np.asarray(inp["sgu_norm_g"][0], f)[None, :], (128, 512))
    b = np.broadcast_to(np.asarray(inp["sgu_norm_b"][0], f)[None, :], (128, 512))
    bs = np.broadcast_to(np.asarray(inp["sgu_b_s"][0], f).reshape(1, 512), (128, 512))
    return np.ascontiguousarray(np.concatenate([ws, g, b, bs], 1))


class Prog:
    def __init__(self, NU, table, sched=None, stage=STAGE):
        self.dry = sched is None
        self.sched = [] if sched is None else sched
        self.NU, self.table, self.stage = NU, table, stage
        self.pos = {}
        if sched is not None:
            order = list(dict.fromkeys(sched)) + [u for u in range(NU) if u not in set(sched)]
            self.order = order
            self.pos = {u: k for k, u in enumerate(order)}
        self.wi = 0
        self.bg_on = False
        self.wissued = 0
        self.psi = 0
        self.kb = None

    def wnext(self, name, idx):
        uid = self.table[name] + idx
        i = self.wi
        self.wi += 1
        if self.dry:
            self.sched.append(uid)
            return 0
        assert self.sched[i] == uid, (i, uid, self.sched[i])
        kb = self.kb
        sched, pos = self.sched, self.pos
        limit = min(len(sched), i + RING - HOLD)
        while self.wissued < limit:
            j0 = self.wissued
            j1 = j0 + 1
            while j1 < len(sched) and j1 % 4 != 0 and pos[sched[j1]] == pos[sched[j1 - 1]] + 1:
                j1 += 1
            if j1 > limit:
                break
            n = j1 - j0
            s0 = j0 % RING
            p0 = pos[sched[j0]]
            kb.dma([self.wbf_b[p0 + k] for k in range(n)], [self.ring_b[s0 + k] for k in range(n)],
                   out=self.ring[:, s0:s0 + n, :],
                   in_=self.wbf[:, p0 * 1024:(p0 + n) * 1024].rearrange("p (u n) -> p u n", u=n),
                   dsem=self.ring_d[s0])
            self.wissued = j1
        return i % RING

    def av(self, dt, off_bytes, dims):
        if dt == F32:
            return bass.AP(self.ar_f32, off_bytes // 4, [[ARENA // 4, 128]] + dims)
        return bass.AP(self.ar_bf, off_bytes // 2, [[ARENA // 2, 128]] + dims)

    def wt(self, slot, k):
        return self.ring[:, slot, k * 128:(k + 1) * 128]

    def ps(self):
        i = self.psi % 8
        self.psi += 1
        return self.psum[:, i, :], self.psum_b[i]

    def build_dry(self):
        self.xT = self.yT = self.xs = self.memT = None
        self.dz_b, self.dyb_b = {}, {}
        self.xs_b = {}
        self.body()

    def build(self):
        import contextlib
        nc = bass.Bass("TRN2", target_bir_lowering=False)
        self.nc = nc
        NU = self.NU
        self.xT = nc.dram_tensor("xT", [D, LT], F32, kind="ExternalInput")
        self.wall = nc.dram_tensor("wall", [128, NU * 1024], F32, kind="ExternalInput")
        self.vecs_d = nc.dram_tensor("vecs", [128, len(VEC_NAMES) * 8], F32, kind="ExternalInput")
        self.memT = nc.dram_tensor("memT", [D, 2 * NMEM], F32, kind="ExternalInput")
        self.yT = nc.dram_tensor("yT", [D, LT], F32, kind="ExternalOutput")
        self.s5p_d = nc.dram_tensor("s5p", [128, S5C["N"]], F32, kind="ExternalInput")
        self.sgup_d = nc.dram_tensor("sgup", [128, 2048], F32, kind="ExternalInput")
        self.Dzn = nc.dram_tensor("Dzn", [4, 128, LT], BF16)
        self.Dgn = nc.dram_tensor("Dgn", [4, 128, LT], BF16)
        self.Dz = nc.dram_tensor("Dz", [4, 128, TB, NB], BF16)
        self.Dyb = nc.dram_tensor("Dyb", [4, 128, LT], BF16)
        self.Dg = nc.dram_tensor("Dg", [4, TB, 128, NB], BF16)
        self.Smat = nc.dram_tensor("Smat", [16, 128, 3072], BF16)
        self.dz_b, self.dyb_b, self.dg_b, self.smat_b = {}, {}, Buf(), [Buf() for _ in range(16)]
        self.wbf = nc.dram_tensor("wbf", [128, NU * 1024], BF16)
        self.xs = nc.dram_tensor("xs", [D, LT], F32)
        self.wbf_b = [Buf() for _ in range(NU)]
        self.xs_b = {}
        with contextlib.ExitStack() as st:
            def sb(name, shape, dt):
                return st.enter_context(nc.sbuf_tensor(name, shape, dt))
            self.ar_bf = sb("arena", [128, ARENA // 2], BF16)
            self.ar_f32 = self.ar_bf.bitcast(F32)
            self.x = self.av(F32, OFF_X, [[XW, DT], [1, XW]])
            self.h = self.av(BF16, OFF_H, [[XW, DT], [1, XW]])
            self.act = self.av(BF16, OFF_ACT, [[NT, 32], [1, NT]])
            self.ring = self.av(BF16, OFF_RING, [[1024, RING], [1, 1024]])
            self.sq = sb("sq", [128, DT, 512], BF16)
            self.rs = sb("rs", [128, XW], F32)
            self.sg = sb("sg", [128, 4, 512], BF16)
            self.vecs = sb("vecs_sb", [128, len(VEC_NAMES) * 8], F32)
            self.ones = sb("ones", [128, 128], BF16)
            self.epsc = sb("epsc", [128, 1], F32)
            self.ktb = sb("ktb", [128, 2, DT, NMEM], BF16)
            self.vvb = sb("vvb", [128, 2, 2, D], BF16)
            self.rden = sb("rden", [128, 2, 512], F32)
            self.gbt = sb("gbt", [128, 2, 512], F32)
            self.wst = sb("wst", [128, 512], BF16)
            self.bsr = sb("bsr", [1, 512], BF16)
            self.lam16 = sb("lam16", [128, 2, 2, 32], F32)
            self.lnst = sb("lnst", [128, 8, 16], F32)
            self.lnst_b = [Buf(), Buf()]
            self.stt = sb("stt", [128, 512], F32)
            self.bgst = sb("bgst", [128, 4096], F32)
            self.kv_b = [[Buf() for _ in range(2)] for _ in range(2)]
            self.rden_b = [Buf(), Buf()]
            self.rdi = 0
            self.pbi = 0
            self.psum = st.enter_context(nc.psum_tensor("psum", [128, 8, 512], F32))
            sems = [st.enter_context(nc.semaphore(f"s{i}")) for i in range(64)]
            self.kb = KB(nc, sems)
            self.ring_b = [Buf() for _ in range(RING)]
            self.ring_d = [self.kb.new_dsem() for _ in range(RING)]
            self.psum_b = [Buf() for _ in range(8)]
            self.x_b = [[Buf() for _ in range(3)] for _ in range(DT)]
            self.h_b = [[Buf() for _ in range(3)] for _ in range(DT)]
            self.act_b = [[Buf() for _ in range(2)] for _ in range(32)]
            self.sq_b = [Buf() for _ in range(DT)]
            self.rs_b = [Buf() for _ in range(3)]
            self.sg_b = [Buf() for _ in range(4)]
            self.sgi = 0
            self.c_b = Buf()
            self.body()
            self.kb.barrier()
            self.kb.emit()
        return nc

    def vcol(self, name, dt):
        j = VEC_NAMES.index(name) * 8 + dt
        return self.vecs[:, j:j + 1]

    CH = [(0, HALO, 512), (1, HALO + 512, 512)]

    def body(self):
        kb = self.kb
        if not self.dry:
            kb.dma([], [self.c_b], out=self.vecs[:], in_=self.vecs_d.ap())
            kb.op("pool", [], [self.c_b], "memset", ap=self.ones[:], constant=1.0)
            kb.op("pool", [], [self.c_b], "memset", ap=self.epsc[:], constant=EPS)
            self.precast_init()
            self.small_setup()
            self.s5_setup()
            if self.stage != 99:
                self.bg_flush()
        tiles = [(t0, LP) for t0 in range(0, LP, NT)] + [(LP + t0, LS) for t0 in range(0, LS, NT)]
        if self.stage == 2:
            self.kv_setup(0, 0)
            self.load_x(self.xT, None, 0)
            self.cross(0, 0)
            self.store_x(self.yT, None, 0)
            return
        if self.stage == 3:
            for t0 in (0, LP - NT):
                if not self.dry:
                    le, re = self.load_x_halo(self.xT, [], t0, 0, LP)
                else:
                    le = re = False
                self.pool_mix(le, re)
                self.store_x(self.yT, None, t0)
            return
        seqs = [(0, 0, LP), (1, LP, LS)]
        alltiles = [(sq_, t0, s0, sl) for (sq_, s0, sl) in seqs for t0 in range(s0, s0 + sl, NT)]
        if self.stage == 97:
            if not self.dry:
                bb = Buf()
                for i in range(60):
                    kb.dma([bb], [bb], out=self.xs.ap(), in_=self.xT.ap())
                kb.barrier()
        if self.stage in (5, 97):
            for (sq_, s0, sl) in seqs:
                self.kv_setup(1, sq_)
            for (sq_, t0, s0, sl) in alltiles:
                self.phaseE_tile(self.xT, {}, sq_, t0, s0, sl)
            return
        if self.stage == 6:
            for (sq_, t0, s0, sl) in alltiles:
                self.load_x(self.xT, None, t0)
                self.mixA(t0)
                self.store_x(self.xs, self.xs_b, t0)
            self.s5_scan()
            for (sq_, t0, s0, sl) in alltiles:
                self.load_x(self.xs, self.xs_b, t0)
                self.mixC(t0)
                self.store_x(self.yT, None, t0)
            return
        if self.stage == 98:
            alltiles = alltiles[:4]
        self.bg_on = True
        self.load_x(self.xT, None, alltiles[0][1])
        for ti, (sq_, t0, s0, sl) in enumerate(alltiles):
            nxt_t0 = alltiles[ti + 1][1] if ti + 1 < len(alltiles) else None
            self.ffn("ffn1", 0, after_d=lambda d, t0=t0: self.store_x_dt(self.xs, self.xs_b, t0, d))

            def pre(nxt_t0=nxt_t0):
                if nxt_t0 is not None:
                    for dt in range(DT):
                        self.load_x_dt(self.xT, None, nxt_t0, dt)
            self.mixA(t0, after_norm=pre)
        self.bg_on = False
        self.bg_flush()
        self.s5_scan()
        for (sq_, s0, sl) in seqs:
            self.kv_setup(0, sq_)
        self.load_x(self.xs, self.xs_b, alltiles[0][1])
        for ti, (sq_, t0, s0, sl) in enumerate(alltiles):
            nxt_t0 = alltiles[ti + 1][1] if ti + 1 < len(alltiles) else None
            self.mixC(t0)
            self.cross(0, sq_)
            self.ffn("ffn2", 0)

            def swap(d, t0=t0, nxt_t0=nxt_t0):
                self.store_x_dt(self.xs, self.xs_b, t0, d)
                if nxt_t0 is not None:
                    self.load_x_dt(self.xs, self.xs_b, nxt_t0, d)
            self.ffn("ffn1", 1, after_d=swap)
        for (sq_, s0, sl) in seqs:
            self.kv_setup(1, sq_)
        for (sq_, t0, s0, sl) in alltiles:
            self.phaseE_tile(self.xs, self.xs_b, sq_, t0, s0, sl)

    def phaseE_tile(self, src, src_b, sq_, t0, s0, sl):
        if not self.dry:
            sb_ = list({id(src_b[(t, dt)]): src_b[(t, dt)] for t in (t0 - NT, t0, t0 + NT) for dt in range(DT)
                        if (t, dt) in src_b}.values())
            le, re = self.load_x_halo(src, sb_, t0, s0, sl)
        else:
            le = re = False
        self.pool_mix(le, re)
        self.cross(1, sq_)
        self.ffn("ffn2", 1)
        self.rmsnorm("final_norm", to_act=True)
        if not self.dry:
            self.kb.dma([b for r in range(16) for b in self.act_b[r]], [],
                        out=self.yT.ap().rearrange("(dt p) t -> p dt t", p=128)[:, :, t0:t0 + NT],
                        in_=self.av(F32, OFF_ACT, [[NT, DT], [1, NT]]))

    def mixA(self, t0, after_norm=None):
        kb = self.kb
        self.rmsnorm("mix_norm0")
        if after_norm is not None:
            after_norm()
        col0 = t0 // TB
        for mt in range(8):
            s = self.wnext("win_a", mt)
            if self.dry:
                continue
            for (c, c0, cn) in self.CH:
                pt, pb = self.ps()
                kb.group("pe", [self.h_b[kt][c] for kt in range(DT)] + [self.ring_b[s]], [pb],
                         [("matmul", dict(out=pt, lhsT=self.wt(s, kt), rhs=self.h[:, kt, c0:c0 + cn],
                                          start=(kt == 0), stop=(kt == DT - 1))) for kt in range(DT)])
                if mt < 4:
                    kb.op("act", [pb], [self.act_b[8 + mt][c]], "copy", out=self.arow(8 + mt, c * 512, 512), in_=pt)
                else:
                    kb.op("act", [pb], [self.act_b[mt - 4][c]], "activation",
                          out=self.arow(mt - 4, c * 512, 512), in_=pt, func=AF.Gelu_apprx_tanh)
        if not self.dry:
            kb.dma([b for r in range(8, 12) for b in self.act_b[r]], [],
                   out=self.Dzn.ap().rearrange("ct p t -> p ct t")[:, :, t0:t0 + NT],
                   in_=self.av(BF16, OFF_ACT + 8 * NT * 2, [[NT, 4], [1, NT]]))
        sl = [self.wnext("win_v", q) for q in range(4)]
        if self.dry:
            return
        NCH = NT // 128
        vgs = [self.arow_f32(12 + j8, 0, 512) for j8 in range(NCH)]
        stb = [self.lnst_b[0]]
        for j8 in range(NCH):
            c = j8 // 4
            pt, pb = self.ps()
            t_lo = HALO + 128 * j8
            kb.group("pe", [self.h_b[kt][c] for kt in range(DT)] + [self.ring_b[s] for s in sl], [pb],
                     [("matmul", dict(out=pt, lhsT=self.h[:, kt, t_lo:t_lo + 128],
                                      rhs=self.ring[:, sl[kt // 2], (kt % 2) * 512:(kt % 2) * 512 + 512],
                                      start=(kt == 0), stop=(kt == DT - 1))) for kt in range(DT)])
            vg, vgb = vgs[j8]
            st = self.lnst[:, j8, :]
            kb.op("act", [pb], vgb, "activation", out=vg, in_=pt, func=AF.Gelu_apprx_tanh)
            kb.op("dve", vgb, stb, "bn_stats", out=st[:, 0:6], in_=vg)
            kb.op("dve", stb, stb, "bn_aggr", out=st[:, 6:8], in_=st[:, 0:6])
        kb.op("act", stb + [self.c_b], stb, "activation", out=self.lnst[:, :, 8], in_=self.lnst[:, :, 7], func=AF.Sqrt,
              bias=self.epsc[:], scale=1.0)
        kb.op("dve", stb, stb, "reciprocal", out=self.lnst[:, :, 8], in_=self.lnst[:, :, 8])
        for j8 in range(NCH):
            c = j8 // 4
            k = j8 % 2
            vg, vgb = vgs[j8]
            st = self.lnst[:, j8, :]
            kb.op("dve", vgb + stb, vgb, "tensor_scalar", out=vg, in0=vg, scalar1=st[:, 6:7], scalar2=st[:, 8:9],
                  op0=ALU.subtract, op1=ALU.mult)
            kb.op("pool", vgb + [self.c_b], vgb, "tensor_tensor", out=vg, in0=vg, in1=self.gbt[:, 0, :], op=ALU.mult)
            vln = self.arow(20, k * 512, 512)
            vlb = [self.act_b[20][k]]
            kb.op("pool", vgb + [self.c_b], vlb, "tensor_tensor", out=vln, in0=vg, in1=self.gbt[:, 1, :], op=ALU.add)
            pt2, pb2 = self.ps()
            calls = []
            for hd in range(4):
                calls.append(("matmul", dict(out=pt2[:, hd * 128:(hd + 1) * 128], lhsT=vln[:, hd * 128:(hd + 1) * 128],
                                             rhs=self.wst[:, hd * 128:(hd + 1) * 128], start=True, stop=False)))
                calls.append(("matmul", dict(out=pt2[:, hd * 128:(hd + 1) * 128], lhsT=self.ones[0:1, :],
                                             rhs=self.bsr[0:1, hd * 128:(hd + 1) * 128], start=False, stop=True)))
            kb.group("pe", vlb + [self.c_b], [pb2], calls)
            ybo = self.av(BF16, OFF_ACT + (4 * NT + 128 * j8) * 2, [[NT, 4], [1, 128]])
            ubi = self.av(BF16, OFF_ACT + (128 * j8) * 2, [[NT, 4], [1, 128]])
            kb.op("dve", [pb2] + [self.act_b[r][c] for r in range(4)], [self.act_b[4 + r][c] for r in range(4)],
                  "tensor_tensor", out=ybo, in0=pt2.rearrange("p (a b) -> p a b", a=4), in1=ubi, op=ALU.mult)
        self.dyb_b[t0] = Buf()
        kb.dma([b for r in range(4, 8) for b in self.act_b[r]], [self.dyb_b[t0]],
               out=self.Dyb.ap().rearrange("ct p t -> p ct t")[:, :, t0:t0 + NT],
               in_=self.av(BF16, OFF_ACT + 4 * NT * 2, [[NT, 4], [1, NT]]))

    def mixC(self, t0):
        kb = self.kb
        col0 = t0 // TB
        if not self.dry:
            kb.dma([], [b for r in range(0, 4) for b in self.act_b[r]],
                   out=self.av(BF16, OFF_ACT, [[NT, 4], [1, NT]]),
                   in_=self.Dgn.ap().rearrange("ct p t -> p ct t")[:, :, t0:t0 + NT])
            kb.dma([self.dyb_b[t0]], [b for r in range(8, 12) for b in self.act_b[r]],
                   out=self.av(BF16, OFF_ACT + 8 * NT * 2, [[NT, 4], [1, NT]]),
                   in_=self.Dyb.ap().rearrange("ct p t -> p ct t")[:, :, t0:t0 + NT])
        for u in range(2):
            s = self.wnext("wglu", u)
            if self.dry:
                continue
            for ml in range(2):
                mt = 2 * u + ml
                for c in range(2):
                    pt, pb = self.ps()
                    kb.group("pe", [self.act_b[kt][c] for kt in range(4)] + [self.ring_b[s]], [pb],
                             [("matmul", dict(out=pt, lhsT=self.wt(s, ml * 4 + kt), rhs=self.arow(kt, c * 512, 512),
                                              start=(kt == 0), stop=(kt == 3))) for kt in range(4)])
                    si = self.sgi % 4
                    self.sgi += 1
                    kb.op("act", [pb], [self.sg_b[si]], "activation", out=self.sg[:, si, :], in_=pt, func=AF.Sigmoid)
                    kb.op("dve", [self.act_b[mt][c], self.sg_b[si]], [self.act_b[4 + mt][c]], "tensor_tensor",
                          out=self.arow(4 + mt, c * 512, 512), in0=self.arow(mt, c * 512, 512), in1=self.sg[:, si, :],
                          op=ALU.mult)
        for dt in range(DT):
            s = self.wnext("wout", dt)
            if self.dry:
                continue
            for (c, c0, cn) in self.CH:
                pt, pb = self.ps()
                rd = [self.act_b[4 + r][c] for r in range(8)]
                kb.group("pe", rd + [self.ring_b[s]], [pb],
                         [("matmul", dict(out=pt, lhsT=self.wt(s, kt), rhs=self.arow(4 + kt, c * 512, 512),
                                          start=(kt == 0), stop=(kt == DT - 1))) for kt in range(DT)])
                kb.op("dve", [pb, self.x_b[dt][c]], [self.x_b[dt][c]], "tensor_tensor",
                      out=self.x[:, dt, c0:c0 + cn], in0=pt, in1=self.x[:, dt, c0:c0 + cn], op=ALU.add)

    def small_setup(self):
        kb = self.kb
        tmp = self.av(F32, 0, [[1, 2048]])
        tb = Buf()
        kb.dma([], [tb], out=tmp, in_=self.sgup_d.ap())
        kb.op("dve", [tb], [self.c_b], "tensor_copy", out=self.wst[:], in_=tmp[:, 0:512])
        kb.op("dve", [tb], [self.c_b], "tensor_copy", out=self.gbt[:, 0, :], in_=tmp[:, 512:1024])
        kb.op("dve", [tb], [self.c_b], "tensor_copy", out=self.gbt[:, 1, :], in_=tmp[:, 1024:1536])
        kb.op("dve", [tb], [self.c_b], "tensor_copy", out=self.bsr[0:1, :], in_=tmp[0:1, 1536:2048])
        kb.barrier()

    def s5_setup(self):
        kb = self.kb
        C = S5C
        off = [0]

        def T(n, esz=4):
            o = off[0]
            off[0] += ((n * esz + 3) // 4) * 4
            return o

        def f(o, dims):
            return self.av(F32, o, dims)

        NK = 2 * TB + 1
        o_p5 = T(C["N"])
        b_p5 = Buf()
        kb.dma([], [b_p5], out=f(o_p5, [[1, C["N"]]]), in_=self.s5p_d.ap())

        def p5(name, dims):
            return f(o_p5 + 4 * C[name], dims)
        o_kvi, o_kv, o_one, o_id = T(NK), T(NK), T(256), T(128)
        o_mask = T(4 * 256)
        b_c = Buf()
        kvi = bass.AP(self.ar_bf.bitcast(I32), o_kvi // 4, [[ARENA // 4, 128], [1, NK]])
        kb.op("pool", [], [b_c], "iota", out=kvi, pattern=[[1, NK]], base=-TB, channel_multiplier=0)
        kb.op("dve", [b_c], [b_c], "tensor_copy", out=f(o_kv, [[1, NK]]), in_=kvi)
        kb.op("pool", [], [b_c], "memset", ap=f(o_one, [[1, 256]]), constant=1.0)
        kb.op("pool", [b_c], [b_c], "affine_select", out=f(o_id, [[1, 128]]), in_=f(o_one, [[1, 128]]),
              pattern=[[1, 128]], compare_op=ALU.is_equal, fill=0.0, base=0, channel_multiplier=-1)
        for ks in range(2):
            kb.op("pool", [b_c], [b_c], "affine_select", out=f(o_mask + ks * 1024, [[16, 16], [1, 16]]),
                  in_=f(o_one, [[16, 16], [1, 16]]), pattern=[[16, 16], [0, 16]], compare_op=ALU.is_ge, fill=0.0,
                  base=15 - 128 * ks, channel_multiplier=-1)
            kb.op("pool", [b_c], [b_c], "affine_select", out=f(o_mask + 2048 + ks * 1024, [[16, 16], [1, 16]]),
                  in_=f(o_one, [[16, 16], [1, 16]]), pattern=[[-16, 16], [0, 16]], compare_op=ALU.is_ge, fill=0.0,
                  base=128 * ks, channel_multiplier=1)
        o_dt, o_a, o_th = T(32), T(32), T(32)
        b_s = Buf()
        kb.op("act", [b_p5], [b_s], "activation", out=f(o_dt, [[1, 32]]), in_=p5("LDT", [[1, 32]]), func=AF.Exp)
        kb.op("dve", [b_p5, b_s], [b_s], "tensor_tensor", out=f(o_a, [[1, 32]]), in0=p5("LR", [[1, 32]]),
              in1=f(o_dt, [[1, 32]]), op=ALU.mult)
        kb.op("dve", [b_p5, b_s], [b_s], "tensor_tensor", out=f(o_th, [[1, 32]]), in0=p5("LI", [[1, 32]]),
              in1=f(o_dt, [[1, 32]]), op=ALU.mult)
        NP_ = 32 * NK
        o_mag, o_ang, o_v, o_vi, o_vf, o_m, o_pwr, o_pwi = [T(NP_) for _ in range(8)]
        d3 = [[NK, 32], [1, NK]]
        b_t = Buf()
        kb.op("dve", [b_s, b_c], [b_t], "tensor_tensor", out=f(o_mag, d3), in0=f(o_a, [[1, 32], [0, NK]]),
              in1=f(o_kv, [[0, 32], [1, NK]]), op=ALU.mult)
        kb.op("act", [b_t], [b_t], "activation", out=f(o_mag, d3), in_=f(o_mag, d3), func=AF.Exp)
        kb.op("dve", [b_s, b_c], [b_t], "tensor_tensor", out=f(o_ang, d3), in0=f(o_th, [[1, 32], [0, NK]]),
              in1=f(o_kv, [[0, 32], [1, NK]]), op=ALU.mult)
        vi = bass.AP(self.ar_bf.bitcast(I32), o_vi // 4, [[ARENA // 4, 128]] + d3)
        for (phase, o_out) in ((0.25, o_pwr), (0.0, o_pwi)):
            kb.op("dve", [b_t], [b_t], "tensor_scalar", out=f(o_v, d3), in0=f(o_ang, d3),
                  scalar1=1.0 / (2 * math.pi), scalar2=256.0 + phase, op0=ALU.mult, op1=ALU.add)
            kb.op("dve", [b_t], [b_t], "tensor_copy", out=vi, in_=f(o_v, d3))
            kb.op("dve", [b_t], [b_t], "tensor_copy", out=f(o_vf, d3), in_=vi)
            kb.op("dve", [b_t], [b_t], "tensor_tensor", out=f(o_v, d3), in0=f(o_v, d3), in1=f(o_vf, d3), op=ALU.subtract)
            kb.op("dve", [b_t], [b_t], "tensor_single_scalar", out=f(o_m, d3), in_=f(o_v, d3), scalar=0.5, op=ALU.is_gt)
            kb.op("dve", [b_t], [b_t], "tensor_tensor", out=f(o_v, d3), in0=f(o_v, d3), in1=f(o_m, d3), op=ALU.subtract)
            kb.op("dve", [b_t], [b_t], "tensor_single_scalar", out=f(o_m, d3), in_=f(o_v, d3), scalar=-0.5, op=ALU.is_lt)
            kb.op("dve", [b_t], [b_t], "tensor_tensor", out=f(o_v, d3), in0=f(o_v, d3), in1=f(o_m, d3), op=ALU.add)
            kb.op("act", [b_t], [b_t], "activation", out=f(o_v, d3), in_=f(o_v, d3), func=AF.Sin, scale=6.28318)
            kb.op("dve", [b_t], [b_t], "tensor_tensor", out=f(o_out, d3), in0=f(o_v, d3), in1=f(o_mag, d3), op=ALU.mult)
        k1 = TB + 1
        o_nr, o_den, o_t1, o_t2, o_qr, o_qi = [T(32) for _ in range(6)]
        v32 = [[1, 32]]
        pw1r, pw1i = f(o_pwr + 4 * k1, [[NK, 32]]), f(o_pwi + 4 * k1, [[NK, 32]])
        LR, LI = p5("LR", v32), p5("LI", v32)
        b_q = Buf()
        kb.op("dve", [b_t], [b_q], "tensor_scalar", out=f(o_nr, v32), in0=pw1r, scalar1=-1.0, scalar2=None, op0=ALU.add)
        kb.op("dve", [b_p5], [b_q], "tensor_tensor", out=f(o_den, v32), in0=LR, in1=LR, op=ALU.mult)
        kb.op("dve", [b_p5], [b_q], "tensor_tensor", out=f(o_t1, v32), in0=LI, in1=LI, op=ALU.mult)
        kb.op("dve", [b_q], [b_q], "tensor_tensor", out=f(o_den, v32), in0=f(o_den, v32), in1=f(o_t1, v32), op=ALU.add)
        kb.op("dve", [b_q], [b_q], "reciprocal", out=f(o_den, v32), in_=f(o_den, v32))
        kb.op("dve", [b_q, b_p5], [b_q], "tensor_tensor", out=f(o_t1, v32), in0=f(o_nr, v32), in1=LR, op=ALU.mult)
        kb.op("dve", [b_q, b_p5, b_t], [b_q], "tensor_tensor", out=f(o_t2, v32), in0=pw1i, in1=LI, op=ALU.mult)
        kb.op("dve", [b_q], [b_q], "tensor_tensor", out=f(o_t1, v32), in0=f(o_t1, v32), in1=f(o_t2, v32), op=ALU.add)
        kb.op("dve", [b_q], [b_q], "tensor_tensor", out=f(o_qr, v32), in0=f(o_t1, v32), in1=f(o_den, v32), op=ALU.mult)
        kb.op("dve", [b_q, b_p5, b_t], [b_q], "tensor_tensor", out=f(o_t1, v32), in0=pw1i, in1=LR, op=ALU.mult)
        kb.op("dve", [b_q, b_p5], [b_q], "tensor_tensor", out=f(o_t2, v32), in0=f(o_nr, v32), in1=LI, op=ALU.mult)
        kb.op("dve", [b_q], [b_q], "tensor_tensor", out=f(o_t1, v32), in0=f(o_t1, v32), in1=f(o_t2, v32), op=ALU.subtract)
        kb.op("dve", [b_q], [b_q], "tensor_tensor", out=f(o_qi, v32), in0=f(o_t1, v32), in1=f(o_den, v32), op=ALU.mult)
        o_bbr, o_bbi, o_u1, o_u2 = T(512), T(512), T(512), T(512)
        dB = [[16, 32], [1, 16]]
        qrb, qib = f(o_qr, [[1, 32], [0, 16]]), f(o_qi, [[1, 32], [0, 16]])
        BR, BI = p5("BR", dB), p5("BI", dB)
        b_bb = Buf()
        kb.op("dve", [b_q, b_p5], [b_bb], "tensor_tensor", out=f(o_u1, dB), in0=qrb, in1=BR, op=ALU.mult)
        kb.op("dve", [b_q, b_p5], [b_bb], "tensor_tensor", out=f(o_u2, dB), in0=qib, in1=BI, op=ALU.mult)
        kb.op("dve", [b_bb], [b_bb], "tensor_tensor", out=f(o_bbr, dB), in0=f(o_u1, dB), in1=f(o_u2, dB), op=ALU.subtract)
        kb.op("dve", [b_q, b_p5, b_bb], [b_bb], "tensor_tensor", out=f(o_u1, dB), in0=qrb, in1=BI, op=ALU.mult)
        kb.op("dve", [b_q, b_p5, b_bb], [b_bb], "tensor_tensor", out=f(o_u2, dB), in0=qib, in1=BR, op=ALU.mult)
        kb.op("dve", [b_bb], [b_bb], "tensor_tensor", out=f(o_bbi, dB), in0=f(o_u1, dB), in1=f(o_u2, dB), op=ALU.add)
        for d in range(2):
            arv = f(o_pwr + 4 * (d * 16 * NK + 2 * TB), [[NK, 16]])
            aiv = f(o_pwi + 4 * (d * 16 * NK + 2 * TB), [[NK, 16]])
            kb.op("dve", [b_t], [self.c_b], "tensor_copy", out=self.lam16[:, d, 0, 0:16], in_=arv)
            kb.op("dve", [b_t], [self.c_b], "tensor_copy", out=self.lam16[:, d, 0, 16:32], in_=arv)
            kb.op("dve", [b_t], [self.c_b], "tensor_scalar", out=self.lam16[:, d, 1, 0:16], in0=aiv, scalar1=-1.0,
                  scalar2=None, op0=ALU.mult)
            kb.op("dve", [b_t], [self.c_b], "tensor_copy", out=self.lam16[:, d, 1, 16:32], in_=aiv)
        NE = 4 * TB * 16
        o_m1, o_m2, o_m3, o_m4 = [T(NE) for _ in range(4)]
        o_X = {nm: (T(NE), T(NE)) for nm in ("A", "P", "Q")}
        o_tmpM = T(4 * 2 * 2 * 256)
        o_tmp2 = T(256)
        o_pack = T(4 * 3072, 2)
        assert off[0] <= ARENA, off[0]
        dX = [[TB * 16, 4], [16, TB], [1, 16]]
        b_m = [Buf() for _ in range(4)]
        b_X = {nm: Buf() for nm in ("A", "P", "Q")}
        b_tmpM, b_tmp2, b_pack = Buf(), Buf(), Buf()
        pack = self.av(BF16, o_pack, [[3072, 4], [1, 3072]])
        ei = [0]

        def ve():
            ei[0] += 1
            return "pool" if ei[0] % 4 == 0 else "dve"
        for ch in range(4):
            for d in range(2):
                dg0 = d * 16 + 4 * ch
                pat = {"A": (2 * TB - 1, -1) if d == 0 else (TB, 1),
                       "P": (TB - 1, -1) if d == 0 else (0, 1),
                       "Q": (TB + 1, 1) if d == 0 else (2 * TB, -1)}
                for nm in ("A", "P", "Q"):
                    st_, sp_ = pat[nm]
                    pwr = f(o_pwr + 4 * (dg0 * NK + st_), [[NK, 4], [sp_, TB], [0, 16]])
                    pwi = f(o_pwi + 4 * (dg0 * NK + st_), [[NK, 4], [sp_, TB], [0, 16]])
                    if nm == "Q":
                        xr = f(o_p5 + 4 * (C["CR"] + dg0 * 16), [[16, 4], [0, TB], [1, 16]])
                        xi = f(o_p5 + 4 * (C["CI"] + dg0 * 16), [[16, 4], [0, TB], [1, 16]])
                        xb = [b_p5]
                    else:
                        xr = f(o_bbr + 4 * dg0 * 16, [[16, 4], [0, TB], [1, 16]])
                        xi = f(o_bbi + 4 * dg0 * 16, [[16, 4], [0, TB], [1, 16]])
                        xb = [b_bb]
                    (orr, oii) = o_X[nm]
                    e1, e2 = ve(), ve()
                    kb.op(e1, [b_t] + xb, [b_m[0]], "tensor_tensor", out=f(o_m1, dX), in0=pwr, in1=xr, op=ALU.mult)
                    kb.op(e1, [b_t] + xb, [b_m[1]], "tensor_tensor", out=f(o_m2, dX), in0=pwi, in1=xi, op=ALU.mult)
                    kb.op(e1, [b_m[0], b_m[1]], [b_X[nm]], "tensor_tensor", out=f(orr, dX), in0=f(o_m1, dX),
                          in1=f(o_m2, dX), op=ALU.subtract)
                    kb.op(e2, [b_t] + xb, [b_m[2]], "tensor_tensor", out=f(o_m3, dX), in0=pwr, in1=xi, op=ALU.mult)
                    kb.op(e2, [b_t] + xb, [b_m[3]], "tensor_tensor", out=f(o_m4, dX), in0=pwi, in1=xr, op=ALU.mult)
                    kb.op(e2, [b_m[2], b_m[3]], [b_X[nm]], "tensor_tensor", out=f(oii, dX), in0=f(o_m3, dX),
                          in1=f(o_m4, dX), op=ALU.add)
                    if nm == "Q":
                        kb.op("dve", [b_X[nm]], [b_X[nm]], "tensor_scalar", out=f(oii, dX), in0=f(oii, dX),
                              scalar1=-1.0, scalar2=None, op0=ALU.mult)
                for ri in range(2):
                    kb.op("act", [b_X["Q"]], [b_pack], "copy",
                          out=self.av(BF16, o_pack + 2 * (1024 + (d * 2 + ri) * 256), [[3072, 4], [1, 256]]),
                          in_=f(o_X["Q"][ri], [[256, 4], [1, 256]]))
                for gi in range(4):
                    for gpar in range(2):
                        g = 2 * (4 * ch + gi) + gpar
                        ps_ = slice(gpar * 64, gpar * 64 + 64)
                        self.bg_step("act")
                        for ks in range(2):
                            Ar = f(o_X["A"][0] + 4 * (gi * 256 + ks * 128), [[1, 128]])[ps_, :]
                            Ai = f(o_X["A"][1] + 4 * (gi * 256 + ks * 128), [[1, 128]])[ps_, :]
                            Pr = f(o_X["P"][0] + 4 * (gi * 256 + ks * 128), [[1, 128]])[ps_, :]
                            Pi = f(o_X["P"][1] + 4 * (gi * 256 + ks * 128), [[1, 128]])[ps_, :]
                            Qr = f(o_X["Q"][0] + 4 * (gi * 256), [[1, 256]])[ps_, :]
                            Qi = f(o_X["Q"][1] + 4 * (gi * 256), [[1, 256]])[ps_, :]
                            idn = f(o_id, [[1, 128]])[ps_, gpar * 64:gpar * 64 + 64]
                            pt, pb = self.ps()
                            kb.group("pe", [b_X["A"], b_c], [pb],
                                     [("matmul", dict(out=pt[:, 0:64], lhsT=Ar, rhs=idn, start=True, stop=True)),
                                      ("matmul", dict(out=pt[:, 64:128], lhsT=Ai, rhs=idn, start=True, stop=True))])
                            kb.op("act", [pb], [b_pack], "copy",
                                  out=self.av(BF16, o_pack + 2 * (gi * 3072 + ((gpar * 2 + d) * 2 + ks) * 128), [[1, 128]]),
                                  in_=pt[:, 0:128])
                            pt, pb = self.ps()
                            kb.group("pe", [b_X["P"], b_X["Q"]], [pb],
                                     [("matmul", dict(out=pt[:, 0:256], lhsT=Pr, rhs=Qr, start=True, stop=False)),
                                      ("matmul", dict(out=pt[:, 0:256], lhsT=Pi, rhs=Qi, start=False, stop=True))])
                            tm = f(o_tmpM + 4 * (((gi * 2 + gpar) * 2 + ks) * 256), [[1, 256]])
                            if d == 0:
                                kb.op("dve", [pb, b_c], [b_tmpM], "tensor_tensor", out=tm, in0=pt[:, 0:256],
                                      in1=f(o_mask + ks * 1024, [[1, 256]]), op=ALU.mult)
                            else:
                                t2 = f(o_tmp2, [[1, 256]])
                                kb.op("dve", [pb, b_c], [b_tmp2], "tensor_tensor", out=t2, in0=pt[:, 0:256],
                                      in1=f(o_mask + 2048 + ks * 1024, [[1, 256]]), op=ALU.mult)
                                kb.op("dve", [b_tmp2, b_tmpM], [b_tmpM], "tensor_tensor", out=tm, in0=tm, in1=t2, op=ALU.add)
                                blk = tm[:, ks * 128:ks * 128 + 128]
                                kb.op("dve", [b_tmpM, b_c, b_p5], [b_tmpM], "scalar_tensor_tensor", out=blk,
                                      in0=f(o_id, [[1, 128]]), scalar=f(o_p5 + 4 * (C["DL"] + g), [[1, 1]]), in1=blk,
                                      op0=ALU.mult, op1=ALU.add)
                                kb.op("act", [b_tmpM], [b_pack], "copy",
                                      out=self.av(BF16, o_pack + 2 * (gi * 3072 + 2048 + (gpar * 2 + ks) * 256), [[1, 256]]),
                                      in_=tm)
            kb.dma([b_pack], self.smat_b[4 * ch:4 * ch + 4],
                   out=self.Smat.ap().rearrange("g p n -> p g n")[:, 4 * ch:4 * ch + 4, :], in_=pack)
        kb.barrier()

    def s5_scan(self):
        if self.dry:
            return
        kb = self.kb
        kb.barrier()
        O_U, O_SH, O_MAT, O_GST, O_GIL, O_GNAT = 0, 49152, 98304, 110592, 122880, 135168
        Uv = self.av(BF16, O_U, [[32 * NB, 2], [NB, 32], [1, NB]])
        SH = self.av(BF16, O_SH, [[2 * 16 * NB, 2], [16 * NB, 2], [NB, 16], [1, NB]])
        mats = [self.av(BF16, O_MAT + k * 6144, [[1, 3072]]) for k in range(2)]
        gsts = [self.av(BF16, O_GST, [[8 * NB, 2], [NB, 8], [1, NB]]),
                bass.AP(self.bgst.bitcast(BF16), 0, [[8192, 128], [8 * NB, 2], [NB, 8], [1, NB]])]
        u_b, mat_b, gst_bs = Buf(), [Buf(), Buf()], [Buf(), Buf()]
        sh_s, sh_h = [Buf(), Buf()], [Buf(), Buf()]
        nats = [self.av(BF16, O_GST, [[1, LT]]), self.av(BF16, O_GNAT, [[1, LT]])]
        blks = [self.av(BF16, O_GIL, [[NB, TB], [1, NB]]), self.av(BF16, O_MAT, [[NB, TB], [1, NB]])]
        nat_b, blk_b, dz_b = [Buf(), Buf()], [Buf(), Buf()], Buf()
        dg_b, gil_b, gn_b = Buf(), Buf(), Buf()
        def b0_in(ct):
            kb.dma([], [nat_b[ct % 2]], out=nats[ct % 2], in_=self.Dzn[ct])

        def b0_rest(ct):
            k = ct % 2
            kb.op(["dve", "act"][k], [nat_b[k]], [blk_b[k]], "tensor_copy" if k == 0 else "copy",
                  out=blks[k], in_=nats[k].rearrange("p (j s) -> p s j", s=TB))
            kb.dma([blk_b[k]], [dz_b], out=self.Dz[ct], in_=blks[k])
        b0_in(0)
        b0_in(1)
        for ct in range(4):
            b0_rest(ct)
            if ct + 2 < 4:
                b0_in(ct + 2)
        for s in range(TB):
            kb.dma([dz_b], [u_b], out=Uv[(s % 8) * 16:(s % 8) * 16 + 16, s // 8, :, :],
                   in_=self.Dz.ap()[:, :, s, :].rearrange("ct (gl c) j -> c (ct gl) j", c=16))
        def load_mats(gp, extra=()):
            kb.dma([self.smat_b[gp]], [mat_b[gp % 2]] + list(extra), out=mats[gp % 2], in_=self.Smat[gp])
        load_mats(0, [blk_b[1]])
        for gp in range(16):
            m = mats[gp % 2]
            if gp + 1 < 16:
                load_mats(gp + 1, [blk_b[1]] if gp == 0 else [])
            for d in range(2):
                for ri in range(2):
                    pt, pb = self.ps()
                    calls = []
                    for gpar in range(2):
                        for ks in range(2):
                            c0 = ((gpar * 2 + d) * 2 + ks) * 128 + ri * 64
                            calls.append(("matmul", dict(out=pt[gpar * 64:gpar * 64 + 64, 0:NB], lhsT=m[:, c0:c0 + 64],
                                                         rhs=Uv[:, ks, 2 * gp + gpar, :], start=(ks == 0), stop=(ks == 1))))
                    kb.group("pe", [u_b, mat_b[gp % 2]], [pb], calls)
                    kb.op("act", [pb], [sh_s[d]], "copy", out=SH[:, d, ri, gp, :], in_=pt[:, 0:NB])
        load_mats(0)
        load_mats(1)

        def st(o, dims=None):
            return bass.AP(self.stt, o, [[512, 128]] + (dims or [[1, 32]]))
        chains = []
        for ci, (lo, hi, en) in enumerate(((0, NBP, "dve"), (NBP, NB, "pool"))):
            base = ci * 256
            v3 = [[32, 2], [16, 2], [1, 16]]
            chains.append(dict(lo=lo, hi=hi, en=en,
                               X=[st(base + 64 * k, v3) for k in range(2)],
                               Xs=[st(base + 64 * k + 16, [[32, 2], [-16, 2], [1, 16]]) for k in range(2)],
                               T1=st(base + 128, v3), T2=st(base + 192, v3), xb=[Buf(), Buf()], tb=[Buf(), Buf()]))
        A1 = self.lam16[:, :, 0, :].rearrange("p d (a b) -> p d a b", a=2)
        A2 = self.lam16[:, :, 1, :].rearrange("p d (a b) -> p d a b", a=2)
        for k in range(NBP):
            for ch in chains:
                lo, hi, en = ch["lo"], ch["hi"], ch["en"]
                if k >= hi - lo:
                    continue
                jf, jb = lo + k, hi - 1 - k
                Sj = self.av(BF16, O_SH + 2 * jf, [[2 * 16 * NB + jb - jf, 2], [16 * NB, 2], [NB, 16]])
                xp, xn = k % 2, (k + 1) % 2
                X, Xs, T1, T2, xb, tb = ch["X"], ch["Xs"], ch["T1"], ch["T2"], ch["xb"], ch["tb"]
                if k == 0:
                    kb.op(en, sh_s, [xb[xn]], "tensor_copy", out=X[xn], in_=Sj)
                    continue
                kb.op(en, [xb[xp], self.c_b], [tb[0]], "tensor_tensor", out=T1, in0=A1, in1=X[xp], op=ALU.mult)
                kb.op(en, [xb[xp], self.c_b], [tb[1]], "tensor_tensor", out=T2, in0=A2, in1=Xs[xp], op=ALU.mult)
                kb.op(en, [tb[0], tb[1]], [tb[0]], "tensor_tensor", out=T1, in0=T1, in1=T2, op=ALU.add)
                kb.op(en, [tb[0]] + sh_s, [xb[xn]], "tensor_tensor", out=X[xn], in0=T1, in1=Sj, op=ALU.add)
                kb.op("act", [xb[xn]], sh_h, "copy", out=Sj, in_=X[xn])
        gil = self.av(BF16, O_GIL, [[NB, TB], [1, NB]])
        gnat = self.av(BF16, O_GNAT, [[1, LT]])
        deferred, nxt = [], []
        dg_bs = [Buf() for _ in range(16)]
        pending_mats = None
        for gp in range(16):
            m = mats[gp % 2]
            if pending_mats is not None:
                load_mats(pending_mats)
            pending_mats = gp + 2 if gp + 2 < 16 else None
            for gpar in range(2):
                g = 2 * gp + gpar
                gl, ct = g % 8, g // 8
                gst, gst_b = gsts[ct % 2], gst_bs[ct % 2]
                ps_ = slice(gpar * 64, gpar * 64 + 64)
                for mt in range(2):
                    pt, pb = self.ps()
                    calls = []
                    for ks in range(2):
                        c0 = 2048 + (gpar * 2 + ks) * 256 + mt * 128
                        calls.append(("matmul", dict(out=pt[:, 0:NB], lhsT=m[:, c0:c0 + 128], rhs=Uv[:, ks, g, :],
                                                     start=(ks == 0), stop=False)))
                    for ri in range(2):
                        c0 = 1024 + (0 * 2 + ri) * 256 + mt * 128
                        for (a, b) in ((1, NBP), (NBP + 1, NB)):
                            calls.append(("matmul", dict(out=pt[:, a:b], lhsT=m[ps_, c0:c0 + 128],
                                                         rhs=SH[ps_, 0, ri, gp, a - 1:b - 1], start=False, stop=False)))
                    for ri in range(2):
                        c0 = 1024 + (1 * 2 + ri) * 256 + mt * 128
                        for (a, b) in ((0, NBP - 1), (NBP, NB - 1)):
                            last = (ri == 1 and a == NBP)
                            calls.append(("matmul", dict(out=pt[:, a:b], lhsT=m[ps_, c0:c0 + 128],
                                                         rhs=SH[ps_, 1, ri, gp, a + 1:b + 1], start=False, stop=last)))
                    kb.group("pe", [u_b, mat_b[gp % 2], sh_h[0], sh_h[1], sh_s[0], sh_s[1]], [pb], calls)
                    kb.op("act", [pb], [gst_b, nat_b[0]], "activation", out=gst[:, mt, gl, :], in_=pt[:, 0:NB],
                          func=AF.Gelu_apprx_tanh)
                if gl == 7:
                    for fn in deferred:
                        fn()
                    deferred = nxt
                    nxt = []
                    for mt in range(2):
                        for t8 in range(8):
                            kb.dma([gst_b], [dg_bs[8 * mt + t8]],
                                   out=self.Dg[ct][8 * mt + t8].rearrange("(gl c) j -> c gl j", c=16),
                                   in_=gst[t8 * 16:t8 * 16 + 16, mt, :, :])

                    def stage2(ct=ct):
                        kb.dma(dg_bs, [gil_b, blk_b[0]], out=gil, in_=self.Dg[ct].rearrange("t p j -> p t j"))
                        kb.op("dve", [gil_b], [gn_b, nat_b[1]], "tensor_copy",
                              out=gnat.rearrange("p (j s) -> p s j", s=TB), in_=gil)
                        nxt.append(lambda ct=ct: kb.dma([gn_b], [], out=self.Dgn[ct], in_=gnat))
                    deferred.append(stage2)
        while deferred or nxt:
            for fn in deferred:
                fn()
            deferred = nxt
            nxt = []
        kb.barrier()
        self.wissued = self.wi

    def precast_init(self):
        NU = self.NU
        self.bg_q = [(p0, min(2, NU - p0)) for p0 in range(0, NU, 2)]
        self.bg_i = 0
        self.bg_in_b = [Buf(), Buf(), Buf()]
        self.bg_out_b = [Buf(), Buf()]

    def bg_step(self, en="act"):
        if self.dry or self.bg_i >= len(self.bg_q) + 2:
            return
        kb = self.kb
        i = self.bg_i
        self.bg_i += 1
        q = self.bg_q
        ktf = self.ktb.bitcast(F32)
        ins = [self.bgst[:, 0:2048], self.bgst[:, 2048:4096], bass.AP(ktf, 0, [[2048, 128], [1, 2048]])]
        vv2 = bass.AP(self.vvb, 0, [[4096, 128], [2048, 2], [1, 2048]])
        if i < len(q):
            p0, n = q[i]
            kb.dma([], [self.bg_in_b[i % 3]], out=ins[i % 3][:, 0:n * 1024], in_=self.wall[:, p0 * 1024:(p0 + n) * 1024])
        if 1 <= i <= len(q):
            p0, n = q[i - 1]
            kb.op(en, [self.bg_in_b[(i - 1) % 3]], [self.bg_out_b[(i - 1) % 2]], "copy" if en == "act" else "tensor_copy",
                  out=vv2[:, (i - 1) % 2, 0:n * 1024], in_=ins[(i - 1) % 3][:, 0:n * 1024])
        if 2 <= i <= len(q) + 1:
            p0, n = q[i - 2]
            kb.dma([self.bg_out_b[(i - 2) % 2]], [self.wbf_b[p0 + k] for k in range(n)],
                   out=self.wbf[:, p0 * 1024:(p0 + n) * 1024], in_=vv2[:, (i - 2) % 2, 0:n * 1024])

    def bg_flush(self):
        if self.dry:
            return
        while self.bg_i < len(self.bg_q) + 2:
            self.bg_step()

    def load_x(self, src, src_b, t0):
        if self.dry:
            return
        kb = self.kb
        srcv = src.ap().rearrange("(dt p) t -> p dt t", p=128)
        wr = [self.x_b[dt][c] for dt in range(DT) for c in range(2)]
        rd = [] if src_b is None else list({id(src_b[(t0, dt)]): src_b[(t0, dt)] for dt in range(DT)}.values())
        kb.dma(rd, wr, out=self.x[:, :, HALO:HALO + NT], in_=srcv[:, :, t0:t0 + NT])

    def store_x(self, dst, dst_b, t0):
        if self.dry:
            return
        kb = self.kb
        dstv = dst.ap().rearrange("(dt p) t -> p dt t", p=128)
        rd = [self.x_b[dt][c] for dt in range(DT) for c in range(2)]
        wr = []
        if dst_b is not None:
            b = Buf()
            for dt in range(DT):
                dst_b[(t0, dt)] = b
            wr = [b]
        kb.dma(rd, wr, out=dstv[:, :, t0:t0 + NT], in_=self.x[:, :, HALO:HALO + NT])

    def load_x_dt(self, src, src_b, t0, dt):
        if self.dry:
            return
        rd = [] if src_b is None else [src_b[(t0, dt)]]
        self.kb.dma(rd, self.x_b[dt][0:2], out=self.x[:, dt, HALO:HALO + NT], in_=src[dt * 128:(dt + 1) * 128, t0:t0 + NT])

    def store_x_dt(self, dst, dst_b, t0, dt):
        if self.dry:
            return
        dst_b[(t0, dt)] = Buf()
        self.kb.dma(self.x_b[dt][0:2], [dst_b[(t0, dt)]], out=dst[dt * 128:(dt + 1) * 128, t0:t0 + NT],
                    in_=self.x[:, dt, HALO:HALO + NT])

    def calc_rstd(self, ranges=None, lnexp=True):
        kb = self.kb
        allx = [b for r in self.x_b for b in r]
        for (c0, cn) in (ranges or [(HALO, 512), (HALO + 512, 512)]):
            pt, pb = self.ps()
            calls = []
            for dt in range(DT):
                kb.op("act", allx, [self.sq_b[dt]], "activation",
                      out=self.sq[:, dt, 0:cn], in_=self.x[:, dt, c0:c0 + cn], func=AF.Square)
                calls.append(("matmul", dict(out=pt[:, 0:cn], lhsT=self.ones[:], rhs=self.sq[:, dt, 0:cn],
                                             start=(dt == 0), stop=(dt == DT - 1))))
            kb.group("pe", self.sq_b + [self.c_b], [pb], calls)
            if lnexp:
                kb.op("act", [pb, self.c_b], self.rs_b, "activation",
                      out=self.rs[:, c0:c0 + cn], in_=pt[:, 0:cn], func=AF.Ln, bias=self.epsc[:], scale=1.0 / D)
                kb.op("act", self.rs_b, self.rs_b, "activation",
                      out=self.rs[:, c0:c0 + cn], in_=self.rs[:, c0:c0 + cn], func=AF.Exp, scale=-0.5)
                continue
            kb.op("act", [pb, self.c_b], self.rs_b, "activation",
                  out=self.rs[:, c0:c0 + cn], in_=pt[:, 0:cn], func=AF.Sqrt,
                  bias=self.epsc[:], scale=1.0 / D)
            kb.op("dve", self.rs_b, self.rs_b, "reciprocal",
                  out=self.rs[:, c0:c0 + cn], in_=self.rs[:, c0:c0 + cn])

    def rmsnorm(self, vname, to_x=False, lnexp=True, to_act=False):
        if self.dry:
            return
        kb = self.kb
        self.calc_rstd(lnexp=lnexp)
        for (c, c0, cn) in self.CH:
            for dt in range(DT):
                if to_act:
                    dst_ap = self.av(F32, OFF_ACT + (2 * dt + c) * NT * 2, [[1, 512]])
                    dst_b = self.act_b[2 * dt + c]
                elif to_x:
                    dst_ap, dst_b = self.x[:, dt, c0:c0 + cn], [self.x_b[dt][c]]
                else:
                    dst_ap, dst_b = self.h[:, dt, c0:c0 + cn], [self.h_b[dt][c]]
                kb.op("dve", [self.x_b[dt][c], self.c_b] + self.rs_b, dst_b,
                      "scalar_tensor_tensor",
                      out=dst_ap, in0=self.x[:, dt, c0:c0 + cn],
                      scalar=self.vcol(vname, dt), in1=self.rs[:, c0:c0 + cn],
                      op0=ALU.mult, op1=ALU.mult)

    def ffn(self, which, layer, after_d=None):
        kb = self.kb
        self.rmsnorm(f"{which}_norm{layer}")
        for f in range(32):
            sg_ = self.wnext(f"{which}_gate{layer}", f)
            su_ = self.wnext(f"{which}_up{layer}", f)
            if self.dry:
                continue
            if self.bg_on:
                self.bg_step()
            for (c, c0, cn) in self.CH:
                pg, pgb = self.ps()
                pu, pub = self.ps()
                hb = [self.h_b[kt][c] for kt in range(DT)]
                kb.group("pe", hb + [self.ring_b[sg_]], [pgb],
                         [("matmul", dict(out=pg, lhsT=self.wt(sg_, kt), rhs=self.h[:, kt, c0:c0 + cn],
                                          start=(kt == 0), stop=(kt == DT - 1))) for kt in range(DT)])
                kb.group("pe", hb + [self.ring_b[su_]], [pub],
                         [("matmul", dict(out=pu, lhsT=self.wt(su_, kt), rhs=self.h[:, kt, c0:c0 + cn],
                                          start=(kt == 0), stop=(kt == DT - 1))) for kt in range(DT)])
                si = self.sgi % 4
                self.sgi += 1
                kb.op("act", [pgb], [self.sg_b[si]], "activation",
                      out=self.sg[:, si, :], in_=pg, func=AF.Silu)
                kb.op("dve", [pub, self.sg_b[si]], [self.act_b[f][c]], "tensor_tensor",
                      out=self.act[:, f, c * 512:(c + 1) * 512], in0=pu, in1=self.sg[:, si, :], op=ALU.mult)
        for d in range(DT):
            sl = [self.wnext(f"{which}_down{layer}", d * 4 + kb4) for kb4 in range(4)]
            if self.dry:
                continue
            for (c, c0, cn) in self.CH:
                pd, pdb = self.ps()
                kb.group("pe", [self.act_b[f][c] for f in range(32)] + [self.ring_b[s] for s in sl], [pdb],
                         [("matmul", dict(out=pd, lhsT=self.wt(sl[f // 8], f % 8),
                                          rhs=self.act[:, f, c * 512:(c + 1) * 512],
                                          start=(f == 0), stop=(f == 31))) for f in range(32)])
                kb.op("dve", [pdb, self.x_b[d][c]], [self.x_b[d][c]], "scalar_tensor_tensor",
                      out=self.x[:, d, c0:c0 + cn], in0=pd, scalar=0.5, in1=self.x[:, d, c0:c0 + cn],
                      op0=ALU.mult, op1=ALU.add)
            if after_d is not None:
                after_d(d)


    def arow(self, r, c0=0, cn=NT):
        return self.act[:, r, c0:c0 + cn]

    def arow_f32(self, r, off, n):
        ap = self.av(F32, OFF_ACT + (r * (NT // 2) + off) * 4, [[1, n]])
        r1 = (r * (NT // 2) + off + n - 1) // (NT // 2)
        bufs = [b for rr in range(r, r1 + 1) for b in self.act_b[rr]]
        return ap, bufs

    def kv_setup(self, layer, seq):
        kb = self.kb
        if not self.dry:
            mf, mfb = self.arow_f32(0, 0, DT * NMEM)
            mfv = mf.rearrange("p (a b) -> p a b", a=DT)
            hm = self.act[:, 4:6, :].rearrange("p r (a b) -> p (r a) b", a=4)
            hmb = self.act_b[4] + self.act_b[5]
            srcv = self.memT.ap().rearrange("(dt p) m -> p dt m", p=128)
            kb.dma([], mfb, out=mfv, in_=srcv[:, :, seq * NMEM:(seq + 1) * NMEM])
            pt, pb = self.ps()
            calls = []
            for dt in range(DT):
                kb.op("act", mfb, [self.sq_b[dt]], "activation", out=self.sq[:, dt, 0:NMEM], in_=mfv[:, dt, :],
                      func=AF.Square)
                calls.append(("matmul", dict(out=pt[:, 0:NMEM], lhsT=self.ones[:], rhs=self.sq[:, dt, 0:NMEM],
                                             start=(dt == 0), stop=(dt == DT - 1))))
            kb.group("pe", self.sq_b + [self.c_b], [pb], calls)
            kb.op("act", [pb, self.c_b], self.rs_b, "activation", out=self.rs[:, 0:NMEM], in_=pt[:, 0:NMEM],
                  func=AF.Sqrt, bias=self.epsc[:], scale=1.0 / D)
            kb.op("dve", self.rs_b, self.rs_b, "reciprocal", out=self.rs[:, 0:NMEM], in_=self.rs[:, 0:NMEM])
            for dt in range(DT):
                kb.op("dve", mfb + self.rs_b + [self.c_b], hmb, "scalar_tensor_tensor",
                      out=hm[:, dt, :], in0=mfv[:, dt, :], scalar=self.vcol(f"mem_norm{layer}", dt),
                      in1=self.rs[:, 0:NMEM], op0=ALU.mult, op1=ALU.mult)
        for dt in range(DT):
            s = self.wnext(f"wk{layer}", dt)
            if self.dry:
                continue
            pt, pb = self.ps()
            kb.group("pe", hmb + [self.ring_b[s]], [pb],
                     [("matmul", dict(out=pt[:, 0:NMEM], lhsT=self.wt(s, kt), rhs=hm[:, kt, :],
                                      start=(kt == 0), stop=(kt == DT - 1))) for kt in range(DT)])
            kb.op("act", [pb], [self.kv_b[seq][0]], "copy", out=self.ktb[:, seq, dt, :], in_=pt[:, 0:NMEM])
        for nb in range(2):
            sl = [self.wnext(f"wv{layer}", nb * 4 + q) for q in range(4)]
            if self.dry:
                continue
            for mt in range(2):
                pt, pb = self.ps()
                kb.group("pe", hmb + [self.ring_b[s] for s in sl], [pb],
                         [("matmul", dict(out=pt, lhsT=hm[:, kt, mt * 128:(mt + 1) * 128],
                                          rhs=self.ring[:, sl[kt // 2], (kt % 2) * 512:(kt % 2) * 512 + 512],
                                          start=(kt == 0), stop=(kt == DT - 1))) for kt in range(DT)])
                kb.op("act", [pb], [self.kv_b[seq][1]], "copy",
                      out=self.vvb[:, seq, mt, nb * 512:(nb + 1) * 512], in_=pt)

    def cross(self, layer, seq):
        kb = self.kb
        self.rmsnorm(f"cross_norm{layer}", lnexp=True)
        for dt in range(DT):
            s = self.wnext(f"wq{layer}", dt)
            if self.dry:
                continue
            for (c, c0, cn) in self.CH:
                pt, pb = self.ps()
                kb.group("pe", [self.h_b[kt][c] for kt in range(DT)] + [self.ring_b[s]], [pb],
                         [("matmul", dict(out=pt, lhsT=self.wt(s, kt), rhs=self.h[:, kt, c0:c0 + cn],
                                          start=(kt == 0), stop=(kt == DT - 1))) for kt in range(DT)])
                kb.op("act", [pb], [self.act_b[dt][c]], "copy", out=self.arow(dt, c * 512, 512), in_=pt)
        if not self.dry:
            for hd in range(4):
                for (c, c0, cn) in self.CH:
                    pr = 16 + (self.pbi % 4)
                    self.pbi += 1
                    pbufs = self.act_b[pr]
                    for mt in range(2):
                        pt, pb = self.ps()
                        kb.group("pe", [self.act_b[2 * hd][c], self.act_b[2 * hd + 1][c], self.kv_b[seq][0]], [pb],
                                 [("matmul", dict(out=pt, lhsT=self.ktb[:, seq, 2 * hd + a, mt * 128:(mt + 1) * 128],
                                                  rhs=self.arow(2 * hd + a, c * 512, 512),
                                                  start=(a == 0), stop=(a == 1))) for a in range(2)])
                        kb.op("act", [pb], [pbufs[mt]], "activation", out=self.arow(pr, mt * 512, 512), in_=pt,
                              func=AF.Exp, scale=1.0 / 16.0)
                    pt, pb = self.ps()
                    kb.group("pe", pbufs + [self.c_b], [pb],
                             [("matmul", dict(out=pt, lhsT=self.ones[:], rhs=self.arow(pr, mt * 512, 512),
                                              start=(mt == 0), stop=(mt == 1))) for mt in range(2)])
                    ri = self.rdi % 2
                    self.rdi += 1
                    kb.op("act", [pb], [self.rden_b[ri]], "activation", out=self.rden[:, ri, :], in_=pt, func=AF.Ln)
                    kb.op("act", [self.rden_b[ri]], [self.rden_b[ri]], "activation", out=self.rden[:, ri, :],
                          in_=self.rden[:, ri, :], func=AF.Exp, scale=-1.0)
                    for a in range(2):
                        dv = 2 * hd + a
                        pt, pb = self.ps()
                        kb.group("pe", pbufs + [self.kv_b[seq][1]], [pb],
                                 [("matmul", dict(out=pt, lhsT=self.vvb[:, seq, mt, dv * 128:(dv + 1) * 128],
                                                  rhs=self.arow(pr, mt * 512, 512),
                                                  start=(mt == 0), stop=(mt == 1))) for mt in range(2)])
                        kb.op("dve", [pb, self.rden_b[ri]], [self.act_b[8 + dv][c]], "tensor_tensor",
                              out=self.arow(8 + dv, c * 512, 512), in0=pt, in1=self.rden[:, ri, :], op=ALU.mult)
        for dt in range(DT):
            s = self.wnext(f"wo{layer}", dt)
            if self.dry:
                continue
            for (c, c0, cn) in self.CH:
                pt, pb = self.ps()
                kb.group("pe", [self.act_b[8 + kt][c] for kt in range(DT)] + [self.ring_b[s]], [pb],
                         [("matmul", dict(out=pt, lhsT=self.wt(s, kt), rhs=self.arow(8 + kt, c * 512, 512),
                                          start=(kt == 0), stop=(kt == DT - 1))) for kt in range(DT)])
                kb.op("dve", [pb, self.x_b[dt][c]], [self.x_b[dt][c]], "tensor_tensor",
                      out=self.x[:, dt, c0:c0 + cn], in0=pt, in1=self.x[:, dt, c0:c0 + cn], op=ALU.add)

    def load_x_halo(self, src, src_bufs, t0, seq0, seqlen):
        kb = self.kb
        allx = [b for r in self.x_b for b in r]
        srcv = src.ap().rearrange("(dt p) t -> p dt t", p=128)
        lo = max(t0 - HALO, seq0)
        hi = min(t0 + NT + HALO, seq0 + seqlen)
        if lo > t0 - HALO:
            kb.op("pool", [], allx, "memset", ap=self.x[:, :, 0:HALO], constant=0.0)
        if hi < t0 + NT + HALO:
            kb.op("pool", [], allx, "memset", ap=self.x[:, :, HALO + NT:XW], constant=0.0)
        kb.dma(src_bufs, allx, out=self.x[:, :, HALO + (lo - t0):HALO + (hi - t0)], in_=srcv[:, :, lo:hi])
        return lo > t0 - HALO, hi < t0 + NT + HALO

    def pool_mix(self, left_edge, right_edge):
        kb = self.kb
        wins = (2, 4, 8, 16)
        if not self.dry:
            allx = [b for r in self.x_b for b in r]
            self.calc_rstd([(0, 512), (512, 512), (1024, XW - 1024)])
            tsets = [[self.arow_f32(8 + 9 * q + 3 * k, 0, XW) for k in range(3)] for q in range(2)]
            for dt in range(DT):
                gi = dt // 2
                w = wins[gi]
                tmps = tsets[dt % 2]
                (xn, xnb) = tmps[0]
                kb.op("dve", allx + self.rs_b + [self.c_b], xnb, "scalar_tensor_tensor", out=xn, in0=self.x[:, dt, :],
                      scalar=self.vcol("mix_norm1", dt), in1=self.rs[:, 0:XW], op0=ALU.mult, op1=ALU.mult)
                en = "dve" if dt % 2 == 0 else "pool"
                cur, curb = tmps[1]
                kb.op(en, xnb, curb, "tensor_tensor", out=cur[:, 1:XW], in0=xn[:, 0:XW - 1], in1=xn[:, 1:XW], op=ALU.add)
                prev, prevb = cur, curb
                lo = 1
                for k, sh in ((2, 1), (3, 2), (4, 4)):
                    if wins[k - 1] > w:
                        break
                    cur, curb = tmps[1 + (k - 1) % 2]
                    lo2 = lo + sh
                    kb.op(en, prevb, curb, "tensor_tensor", out=cur[:, lo2:XW - lo2],
                          in0=prev[:, lo2 - sh:XW - lo2 - sh], in1=prev[:, lo2 + sh:XW - lo2 + sh], op=ALU.add)
                    prev, prevb, lo = cur, curb, lo2
                S = prev
                ob = self.act_b[dt]
                kb.op("dve", prevb + xnb, ob, "scalar_tensor_tensor", out=self.arow(dt), in0=S[:, HALO:HALO + NT],
                      scalar=1.0 / w, in1=xn[:, HALO:HALO + NT], op0=ALU.mult, op1=ALU.subtract)
                if left_edge:
                    for t in range(w // 2):
                        kb.op("dve", prevb + xnb, ob, "scalar_tensor_tensor", out=self.arow(dt, t, 1),
                              in0=S[:, HALO + t:HALO + t + 1], scalar=1.0 / (t + w // 2),
                              in1=xn[:, HALO + t:HALO + t + 1], op0=ALU.mult, op1=ALU.subtract)
                if right_edge:
                    for k in range(1, w // 2):
                        t = NT - k
                        kb.op("dve", prevb + xnb, ob, "scalar_tensor_tensor", out=self.arow(dt, t, 1),
                              in0=S[:, HALO + t:HALO + t + 1], scalar=1.0 / (k + w // 2),
                              in1=xn[:, HALO + t:HALO + t + 1], op0=ALU.mult, op1=ALU.subtract)
        for u in range(2):
            s = self.wnext("wpool", u)
            if self.dry:
                continue
            for gl in range(2):
                gi = 2 * u + gl
                for mt in range(2):
                    do = 2 * gi + mt
                    for (c, c0, cn) in self.CH:
                        pt, pb = self.ps()
                        kb.group("pe", [self.act_b[2 * gi][c], self.act_b[2 * gi + 1][c], self.ring_b[s]], [pb],
                                 [("matmul", dict(out=pt, lhsT=self.wt(s, gl * 4 + mt * 2 + kt),
                                                  rhs=self.arow(2 * gi + kt, c * 512, 512),
                                                  start=(kt == 0), stop=(kt == 1))) for kt in range(2)])
                        kb.op("dve", [pb, self.x_b[do][c], self.c_b], [self.x_b[do][c]], "scalar_tensor_tensor",
                              out=self.x[:, do, c0:c0 + cn], in0=pt, scalar=self.vcol("pool_scale", do),
                              in1=self.x[:, do, c0:c0 + cn], op0=ALU.mult, op1=ALU.add)


def build_program(NU, table, stage=STAGE):
    dry = Prog(NU, table, None, stage)
    dry.build_dry()
    real = Prog(NU, table, dry.sched, stage)
    return real.build(), real.order


def weight_table():
    table, base = {}, 0
    for i in range(2):
        for f in ("ffn1", "ffn2"):
            for nm, n in ((f"{f}_gate{i}", 32), (f"{f}_up{i}", 32), (f"{f}_down{i}", 32)):
                table[nm] = base
                base += n
        for nm, n in ((f"wq{i}", 8), (f"wk{i}", 8), (f"wv{i}", 8), (f"wo{i}", 8)):
            table[nm] = base
            base += n
    for nm, n in (("win_a", 8), ("win_v", 4), ("wglu", 2), ("wout", 8), ("wpool", 2)):
        table[nm] = base
        base += n
    return base, table


def make_inputs(inp, order=None):
    wall, table = pack_weights(inp)
    if order is not None:
        wall = np.ascontiguousarray(wall[np.asarray(order)].transpose(1, 0, 2)).reshape(128, -1)
    vecs = pack_vecs(inp)
    s5p = pack_s5(inp)
    sgup = pack_sgu(inp)
    in_maps = []
    for c in range(8):
        xT = np.ascontiguousarray(
            np.concatenate([np.asarray(inp["x_prompt"][c]), np.asarray(inp["x_sample"][c])], 0).T)
        memT = np.ascontiguousarray(
            np.concatenate([np.asarray(inp["mem_prompt"][c]), np.asarray(inp["mem_sample"][c])], 0).T)
        in_maps.append({"xT": xT, "memT": memT, "wall": wall, "vecs": vecs, "s5p": s5p, "sgup": sgup})
    assert table == weight_table()[1]
    return in_maps, weight_table()[0], table


def kernel(**inp):
    inp = {k: np.asarray(v) for k, v in inp.items()}
    NU, table = weight_table()
    nc, order = build_program(NU, table)
    in_maps, _, _ = make_inputs(inp, order)
    res = run_bass_kernel_spmd(nc, in_maps, core_ids=list(range(8)))
    yp = np.stack([np.ascontiguousarray(res.results[c]["yT"][:, :LP].T) for c in range(8)], 0)
    ys = np.stack([np.ascontiguousarray(res.results[c]["yT"][:, LP:].T) for c in range(8)], 0)
    return (yp.astype(np.float32), ys.astype(np.float32))
```

```python
import math
import numpy as np
import concourse.bass as bass
import concourse.mybir as mybir
from concourse.bass_utils import run_bass_kernel_spmd

F32 = mybir.dt.float32
BF16 = mybir.dt.bfloat16
I32 = mybir.dt.int32
ALU = mybir.AluOpType
AF = mybir.ActivationFunctionType
AX = mybir.AxisListType

D = 1024
DT = 8
DFF = 4096
LP = 4096
LS = 2048
LT = LP + LS
NMEM = 256
NT = 1024
HALO = 8
XW = NT + 2 * HALO
EPS = 1e-6
RING = 16
HOLD = 8

OFF_X = 0
OFF_H = OFF_X + DT * XW * 4
OFF_ACT = OFF_H + DT * XW * 2
OFF_RING = OFF_ACT + 32 * NT * 2
ARENA = OFF_RING + RING * 1024 * 2

TB = 16
NB = LT // TB
NBP = LP // TB
NJ = NT // TB
S5C = dict(LR=0, LI=32, LDT=64, BR=96, BI=608, CR=1120, CI=1632, DL=2144, N=2176)

STAGE = 99


class Buf:
    __slots__ = ("w", "r")

    def __init__(self):
        self.w = None
        self.r = {}


class Eng:
    def __init__(self, name, sem, si):
        self.name = name
        self.sem = sem
        self.si = si
        self.n = 0
        self.prog = []
        self.know = None


class DSem:
    def __init__(self, sem, si):
        self.sem = sem
        self.si = si
        self.count = 0


class KB:
    NS = 64

    def __init__(self, nc, sems):
        self.nc = nc
        self.sems = list(sems)
        self.semlist = []
        self.E = {}
        for nm in ("pe", "act", "dve", "pool", "sp"):
            self.E[nm] = Eng(nm, *self._newsem())
            self.E[nm].know = np.zeros(self.NS, np.int64)
        self.dsems = []
        self.gen = [self.new_dsem() for _ in range(16)]
        self.geni = 0
        self.snap = {}
        self.seq = 0

    def _newsem(self):
        s = self.sems.pop()
        self.semlist.append(s)
        return s, len(self.semlist) - 1

    def new_dsem(self):
        d = DSem(*self._newsem())
        self.dsems.append(d)
        return d

    def _waits(self, eng, reads, writes, extra=()):
        need = {}

        def add(tok):
            if tok is None:
                return
            si, v = tok
            if need.get(si, 0) < v:
                need[si] = v

        for b in reads:
            add(b.w)
        for b in writes:
            add(b.w)
            for t in b.r.items():
                add(t)
        for t in extra:
            add(t)
        toks = sorted(need.items(), key=lambda t: -self.snap[t][0])
        for (si, v) in toks:
            if eng.know[si] >= v:
                continue
            eng.prog.append(("w", si, v))
            np.maximum(eng.know, self.snap[(si, v)][1], out=eng.know)

    def _commit(self, eng, tok, reads, writes):
        si, v = tok
        self.seq += 1
        k = eng.know.copy()
        k[si] = max(k[si], v)
        self.snap[tok] = (self.seq, k)
        for b in reads:
            if b.r.get(si, 0) < v:
                b.r[si] = v
        for b in writes:
            b.w = tok
            b.r = {}

    def op(self, en, reads, writes, meth, **kw):
        eng = self.E[en]
        self._waits(eng, reads, writes)
        eng.n += 1
        eng.prog.append(("i", meth, kw, eng.si, eng.n))
        self._commit(eng, (eng.si, eng.n), reads, writes)

    def group(self, en, reads, writes, calls):
        eng = self.E[en]
        self._waits(eng, reads, writes)
        for (m, kw) in calls[:-1]:
            eng.prog.append(("i", m, kw, None, 0))
        eng.n += 1
        eng.prog.append(("i", calls[-1][0], calls[-1][1], eng.si, eng.n))
        self._commit(eng, (eng.si, eng.n), reads, writes)

    def dma(self, reads, writes, out, in_, dsem=None, q="sp", **kw):
        eng = self.E[q]
        if dsem is None:
            dsem = self.gen[self.geni % len(self.gen)]
            self.geni += 1
        extra = [(dsem.si, dsem.count)] if dsem.count else []
        self._waits(eng, reads, writes, extra)
        dsem.count += 16
        eng.prog.append(("i", "dma_start", dict(out=out, in_=in_, **kw), dsem.si, dsem.count))
        self._commit(eng, (dsem.si, dsem.count), reads, writes)

    def barrier(self):
        for eng in self.E.values():
            for o in self.E.values():
                if o.n and eng.know[o.si] < o.n:
                    eng.prog.append(("w", o.si, o.n))
                    np.maximum(eng.know, self.snap[(o.si, o.n)][1], out=eng.know)
            for d in self.dsems:
                if d.count and eng.know[d.si] < d.count:
                    eng.prog.append(("w", d.si, d.count))
                    np.maximum(eng.know, self.snap[(d.si, d.count)][1], out=eng.know)

    def emit(self):
        nc = self.nc
        hmap = {"pe": "tensor", "act": "scalar", "dve": "vector", "pool": "gpsimd", "sp": "sync"}
        waited = set()
        for eng in self.E.values():
            for it in eng.prog:
                if it[0] == "w":
                    waited.add((it[1], it[2]))
        remap = {}
        comp_si = {eng.si for eng in self.E.values()}
        for eng in self.E.values():
            rank = 0
            for it in eng.prog:
                if it[0] == "i" and it[3] == eng.si:
                    if (eng.si, it[4]) in waited:
                        rank += 1
                        remap[(eng.si, it[4])] = rank
        sl = self.semlist

        def replay(h, prog, attach=True):
            pend = []
            for it in prog:
                if it[0] == "w":
                    si, v = it[1], it[2]
                    if si in comp_si:
                        v = remap[(si, v)]
                    pend.append((sl[si], v))
                    continue
                if it[1] == "dma_start" or not attach:
                    for (s, v) in pend:
                        h.wait_ge(s, v)
                    pend = []
                for (s, v) in pend[:-1]:
                    h.wait_ge(s, v)
                ins = getattr(h, it[1])(**it[2])
                if pend:
                    ins._wait_ge(pend[-1][0], pend[-1][1])
                pend = []
                if it[3] is not None:
                    if it[3] in comp_si:
                        if (it[3], it[4]) in remap:
                            ins.then_inc(sl[it[3]], 1)
                    else:
                        ins.then_inc(sl[it[3]], 16)
            for (s, v) in pend:
                h.wait_ge(s, v)

        with nc.Block() as block:
            for nm, eng in self.E.items():
                getattr(block, hmap[nm])(lambda h, p=eng.prog: replay(h, p))


def units_lhsT(W):
    K_, M_ = W.shape
    a = W.reshape(K_ // 1024, 8, 128, M_ // 128, 128).transpose(3, 0, 2, 1, 4)
    return np.ascontiguousarray(a).reshape(-1, 128, 1024)


def units_rhs(W):
    K_, N_ = W.shape
    a = W.reshape(4, 2, 128, N_ // 512, 512).transpose(3, 0, 2, 1, 4)
    return np.ascontiguousarray(a).reshape(-1, 128, 1024)


def units_small(W, nk, nm):
    a = W.reshape(nk, 128, nm, 128).transpose(2, 0, 1, 3)
    a = a.reshape(nm * nk, 128, 128)
    nu = (nm * nk) // 8
    a = a.reshape(nu, 8, 128, 128).transpose(0, 2, 1, 3)
    return np.ascontiguousarray(a).reshape(nu, 128, 1024)


def pack_pool(pw):
    a = pw.reshape(2, 2, 2, 128, 2, 128)
    a = a.transpose(0, 3, 1, 4, 2, 5)
    return np.ascontiguousarray(a).reshape(2, 128, 1024)


def pack_weights(inp):
    parts, table, base = [], {}, 0

    def add(name, arr):
        nonlocal base
        table[name] = base
        parts.append(arr)
        base += arr.shape[0]

    for i in range(2):
        for f in ("ffn1", "ffn2"):
            add(f"{f}_gate{i}", units_lhsT(inp[f + "_w_gate"][i]))
            add(f"{f}_up{i}", units_lhsT(inp[f + "_w_up"][i]))
            add(f"{f}_down{i}", units_lhsT(inp[f + "_w_down"][i]))
        add(f"wq{i}", units_lhsT(inp["cross_w_q"][i]))
        add(f"wk{i}", units_lhsT(inp["cross_w_kv"][i][:, :D]))
        add(f"wv{i}", units_rhs(inp["cross_w_kv"][i][:, D:]))
        add(f"wo{i}", units_lhsT(inp["cross_w_o"][i]))
    add("win_a", units_lhsT(inp["ab_w_in"][0][:, :1024]))
    add("win_v", units_rhs(inp["ab_w_in"][0][:, 1024:]))
    add("wglu", units_small(inp["s5_w_glu"][0], 4, 4))
    add("wout", units_lhsT(inp["ab_w_out"][0]))
    add("wpool", pack_pool(inp["pool_w"][0]))
    return np.concatenate(parts, 0), table


VEC_NAMES = ["ffn1_norm0", "ffn1_norm1", "mix_norm0", "mix_norm1", "cross_norm0", "cross_norm1",
             "mem_norm0", "mem_norm1", "ffn2_norm0", "ffn2_norm1", "final_norm", "pool_scale"]


def pack_vecs(inp):
    vs = []
    for nm in VEC_NAMES:
        if nm == "final_norm":
            v = inp["final_norm"]
        elif nm == "pool_scale":
            v = inp["pool_scale"][0]
        else:
            v = inp[nm[:-1]][int(nm[-1])]
        vs.append(np.asarray(v, np.float32).reshape(DT, 128).T)
    return np.ascontiguousarray(np.concatenate(vs, 1))


def pack_s5(inp):
    f = np.float32

    def lay_gn(a):
        return np.asarray(a, f).reshape(2, 16, 2, 64).transpose(2, 3, 0, 1).reshape(128, 32)

    def lay_b(a):
        return np.asarray(a, f).reshape(2, 16, 2, 64, 16).transpose(2, 3, 0, 1, 4).reshape(128, 512)

    def lay_c(a):
        return np.asarray(a, f).reshape(2, 16, 2, 16, 64).transpose(2, 4, 0, 1, 3).reshape(128, 512)

    ldt = np.broadcast_to(np.asarray(inp["s5_log_dt"][0], f)[:, :, None], (2, 32, 64))
    dl = np.broadcast_to(np.asarray(inp["s5_d"][0], f).reshape(1, 32, 16), (8, 32, 16)).transpose(0, 2, 1).reshape(128, 32)
    parts = [lay_gn(inp["s5_lambda_re"][0]), lay_gn(inp["s5_lambda_im"][0]), lay_gn(ldt),
             lay_b(inp["s5_b_re"][0]), lay_b(inp["s5_b_im"][0]), lay_c(inp["s5_c_re"][0]), lay_c(inp["s5_c_im"][0]), dl]
    return np.ascontiguousarray(np.concatenate(parts, 1))


def pack_sgu(inp):
    f = np.float32
    ws = np.asarray(inp["sgu_w_s"][0], f).transpose(2, 0, 1).reshape(128, 512)
    g = np.broadcast_to(np.asarray(inp["sgu_norm_g"][0], f)[None, :], (128, 512))
    b = np.broadcast_to(np.asarray(inp["sgu_norm_b"][0], f)[None, :], (128, 512))
    bs = np.broadcast_to(np.asarray(inp["sgu_b_s"][0], f).reshape(1, 512), (128, 512))
    return np.ascontiguousarray(np.concatenate([ws, g, b, bs], 1))


class Prog:
    def __init__(self, NU, table, sched=None, stage=STAGE):
        self.dry = sched is None
        self.sched = [] if sched is None else sched
        self.NU, self.table, self.stage = NU, table, stage
        self.pos = {}
        if sched is not None:
            order = list(dict.fromkeys(sched)) + [u for u in range(NU) if u not in set(sched)]
            self.order = order
            self.pos = {u: k for k, u in enumerate(order)}
        self.wi = 0
        self.bg_on = False
        self.wissued = 0
        self.psi = 0
        self.kb = None

    def wnext(self, name, idx):
        uid = self.table[name] + idx
        i = self.wi
        self.wi += 1
        if self.dry:
            self.sched.append(uid)
            return 0
        assert self.sched[i] == uid, (i, uid, self.sched[i])
        kb = self.kb
        sched, pos = self.sched, self.pos
        limit = min(len(sched), i + RING - HOLD)
        while self.wissued < limit:
            j0 = self.wissued
            j1 = j0 + 1
            while j1 < len(sched) and j1 % 4 != 0 and pos[sched[j1]] == pos[sched[j1 - 1]] + 1:
                j1 += 1
            if j1 > limit:
                break
            n = j1 - j0
            s0 = j0 % RING
            p0 = pos[sched[j0]]
            kb.dma([self.wbf_b[p0 + k] for k in range(n)], [self.ring_b[s0 + k] for k in range(n)],
                   out=self.ring[:, s0:s0 + n, :],
                   in_=self.wbf[:, p0 * 1024:(p0 + n) * 1024].rearrange("p (u n) -> p u n", u=n),
                   dsem=self.ring_d[s0])
            self.wissued = j1
        return i % RING

    def av(self, dt, off_bytes, dims):
        if dt == F32:
            return bass.AP(self.ar_f32, off_bytes // 4, [[ARENA // 4, 128]] + dims)
        return bass.AP(self.ar_bf, off_bytes // 2, [[ARENA // 2, 128]] + dims)

    def wt(self, slot, k):
        return self.ring[:, slot, k * 128:(k + 1) * 128]

    def ps(self):
        i = self.psi % 8
        self.psi += 1
        return self.psum[:, i, :], self.psum_b[i]

    def build_dry(self):
        self.xT = self.yT = self.xs = self.memT = None
        self.dz_b, self.dyb_b = {}, {}
        self.xs_b = {}
        self.body()

    def build(self):
        import contextlib
        nc = bass.Bass("TRN2", target_bir_lowering=False)
        self.nc = nc
        NU = self.NU
        self.xT = nc.dram_tensor("xT", [D, LT], F32, kind="ExternalInput")
        self.wall = nc.dram_tensor("wall", [128, NU * 1024], F32, kind="ExternalInput")
        self.vecs_d = nc.dram_tensor("vecs", [128, len(VEC_NAMES) * 8], F32, kind="ExternalInput")
        self.memT = nc.dram_tensor("memT", [D, 2 * NMEM], F32, kind="ExternalInput")
        self.yT = nc.dram_tensor("yT", [D, LT], F32, kind="ExternalOutput")
        self.s5p_d = nc.dram_tensor("s5p", [128, S5C["N"]], F32, kind="ExternalInput")
        self.sgup_d = nc.dram_tensor("sgup", [128, 2048], F32, kind="ExternalInput")
        self.Dzn = nc.dram_tensor("Dzn", [4, 128, LT], BF16)
        self.Dgn = nc.dram_tensor("Dgn", [4, 128, LT], BF16)
        self.Dz = nc.dram_tensor("Dz", [4, 128, TB, NB], BF16)
        self.Dyb = nc.dram_tensor("Dyb", [4, 128, LT], BF16)
        self.Dg = nc.dram_tensor("Dg", [4, TB, 128, NB], BF16)
        self.Smat = nc.dram_tensor("Smat", [16, 128, 3072], BF16)
        self.dz_b, self.dyb_b, self.dg_b, self.smat_b = {}, {}, Buf(), [Buf() for _ in range(16)]
        self.wbf = nc.dram_tensor("wbf", [128, NU * 1024], BF16)
        self.xs = nc.dram_tensor("xs", [D, LT], F32)
        self.wbf_b = [Buf() for _ in range(NU)]
        self.xs_b = {}
        with contextlib.ExitStack() as st:
            def sb(name, shape, dt):
                return st.enter_context(nc.sbuf_tensor(name, shape, dt))
            self.ar_bf = sb("arena", [128, ARENA // 2], BF16)
            self.ar_f32 = self.ar_bf.bitcast(F32)
            self.x = self.av(F32, OFF_X, [[XW, DT], [1, XW]])
            self.h = self.av(BF16, OFF_H, [[XW, DT], [1, XW]])
            self.act = self.av(BF16, OFF_ACT, [[NT, 32], [1, NT]])
            self.ring = self.av(BF16, OFF_RING, [[1024, RING], [1, 1024]])
            self.sq = sb("sq", [128, DT, 512], BF16)
            self.rs = sb("rs", [128, XW], F32)
            self.sg = sb("sg", [128, 4, 512], BF16)
            self.vecs = sb("vecs_sb", [128, len(VEC_NAMES) * 8], F32)
            self.ones = sb("ones", [128, 128], BF16)
            self.epsc = sb("epsc", [128, 1], F32)
            self.ktb = sb("ktb", [128, 2, DT, NMEM], BF16)
            self.vvb = sb("vvb", [128, 2, 2, D], BF16)
            self.rden = sb("rden", [128, 2, 512], F32)
            self.gbt = sb("gbt", [128, 2, 512], F32)
            self.wst = sb("wst", [128, 512], BF16)
            self.bsr = sb("bsr", [1, 512], BF16)
            self.lam16 = sb("lam16", [128, 2, 2, 32], F32)
            self.lnst = sb("lnst", [128, 8, 16], F32)
            self.lnst_b = [Buf(), Buf()]
            self.stt = sb("stt", [128, 512], F32)
            self.bgst = sb("bgst", [128, 4096], F32)
            self.kv_b = [[Buf() for _ in range(2)] for _ in range(2)]
            self.rden_b = [Buf(), Buf()]
            self.rdi = 0
            self.pbi = 0
            self.psum = st.enter_context(nc.psum_tensor("psum", [128, 8, 512], F32))
            sems = [st.enter_context(nc.semaphore(f"s{i}")) for i in range(64)]
            self.kb = KB(nc, sems)
            self.ring_b = [Buf() for _ in range(RING)]
            self.ring_d = [self.kb.new_dsem() for _ in range(RING)]
            self.psum_b = [Buf() for _ in range(8)]
            self.x_b = [[Buf() for _ in range(3)] for _ in range(DT)]
            self.h_b = [[Buf() for _ in range(3)] for _ in range(DT)]
            self.act_b = [[Buf() for _ in range(2)] for _ in range(32)]
            self.sq_b = [Buf() for _ in range(DT)]
            self.rs_b = [Buf() for _ in range(3)]
            self.sg_b = [Buf() for _ in range(4)]
            self.sgi = 0
            self.c_b = Buf()
            self.body()
            self.kb.barrier()
            self.kb.emit()
        return nc

    def vcol(self, name, dt):
        j = VEC_NAMES.index(name) * 8 + dt
        return self.vecs[:, j:j + 1]

    CH = [(0, HALO, 512), (1, HALO + 512, 512)]

    def body(self):
        kb = self.kb
        if not self.dry:
            kb.dma([], [self.c_b], out=self.vecs[:], in_=self.vecs_d.ap())
            kb.op("pool", [], [self.c_b], "memset", ap=self.ones[:], constant=1.0)
            kb.op("pool", [], [self.c_b], "memset", ap=self.epsc[:], constant=EPS)
            self.precast_init()
            self.small_setup()
            self.s5_setup()
            if self.stage != 99:
                self.bg_flush()
        tiles = [(t0, LP) for t0 in range(0, LP, NT)] + [(LP + t0, LS) for t0 in range(0, LS, NT)]
        if self.stage == 2:
            self.kv_setup(0, 0)
            self.load_x(self.xT, None, 0)
            self.cross(0, 0)
            self.store_x(self.yT, None, 0)
            return
        if self.stage == 3:
            for t0 in (0, LP - NT):
                if not self.dry:
                    le, re = self.load_x_halo(self.xT, [], t0, 0, LP)
                else:
                    le = re = False
                self.pool_mix(le, re)
                self.store_x(self.yT, None, t0)
            return
        seqs = [(0, 0, LP), (1, LP, LS)]
        alltiles = [(sq_, t0, s0, sl) for (sq_, s0, sl) in seqs for t0 in range(s0, s0 + sl, NT)]
        if self.stage == 97:
            if not self.dry:
                bb = Buf()
                for i in range(60):
                    kb.dma([bb], [bb], out=self.xs.ap(), in_=self.xT.ap())
                kb.barrier()
        if self.stage in (5, 97):
            for (sq_, s0, sl) in seqs:
                self.kv_setup(1, sq_)
            for (sq_, t0, s0, sl) in alltiles:
                self.phaseE_tile(self.xT, {}, sq_, t0, s0, sl)
            return
        if self.stage == 6:
            for (sq_, t0, s0, sl) in alltiles:
                self.load_x(self.xT, None, t0)
                self.mixA(t0)
                self.store_x(self.xs, self.xs_b, t0)
            self.s5_scan()
            for (sq_, t0, s0, sl) in alltiles:
                self.load_x(self.xs, self.xs_b, t0)
                self.mixC(t0)
                self.store_x(self.yT, None, t0)
            return
        if self.stage == 98:
            alltiles = alltiles[:4]
        self.bg_on = True
        self.load_x(self.xT, None, alltiles[0][1])
        for ti, (sq_, t0, s0, sl) in enumerate(alltiles):
            nxt_t0 = alltiles[ti + 1][1] if ti + 1 < len(alltiles) else None
            self.ffn("ffn1", 0, after_d=lambda d, t0=t0: self.store_x_dt(self.xs, self.xs_b, t0, d))

            def pre(nxt_t0=nxt_t0):
                if nxt_t0 is not None:
                    for dt in range(DT):
                        self.load_x_dt(self.xT, None, nxt_t0, dt)
            self.mixA(t0, after_norm=pre)
        self.bg_on = False
        self.bg_flush()
        self.s5_scan()
        for (sq_, s0, sl) in seqs:
            self.kv_setup(0, sq_)
        self.load_x(self.xs, self.xs_b, alltiles[0][1])
        for ti, (sq_, t0, s0, sl) in enumerate(alltiles):
            nxt_t0 = alltiles[ti + 1][1] if ti + 1 < len(alltiles) else None
            self.mixC(t0)
            self.cross(0, sq_)
            self.ffn("ffn2", 0)

            def swap(d, t0=t0, nxt_t0=nxt_t0):
                self.store_x_dt(self.xs, self.xs_b, t0, d)
                if nxt_t0 is not None:
                    self.load_x_dt(self.xs, self.xs_b, nxt_t0, d)
            self.ffn("ffn1", 1, after_d=swap)
        for (sq_, s0, sl) in seqs:
            self.kv_setup(1, sq_)
        for (sq_, t0, s0, sl) in alltiles:
            self.phaseE_tile(self.xs, self.xs_b, sq_, t0, s0, sl)

    def phaseE_tile(self, src, src_b, sq_, t0, s0, sl):
        if not self.dry:
            sb_ = list({id(src_b[(t, dt)]): src_b[(t, dt)] for t in (t0 - NT, t0, t0 + NT) for dt in range(DT)
                        if (t, dt) in src_b}.values())
            le, re = self.load_x_halo(src, sb_, t0, s0, sl)
        else:
            le = re = False
        self.pool_mix(le, re)
        self.cross(1, sq_)
        self.ffn("ffn2", 1)
        self.rmsnorm("final_norm", to_act=True)
        if not self.dry:
            self.kb.dma([b for r in range(16) for b in self.act_b[r]], [],
                        out=self.yT.ap().rearrange("(dt p) t -> p dt t", p=128)[:, :, t0:t0 + NT],
                        in_=self.av(F32, OFF_ACT, [[NT, DT], [1, NT]]))

    def mixA(self, t0, after_norm=None):
        kb = self.kb
        self.rmsnorm("mix_norm0")
        if after_norm is not None:
            after_norm()
        col0 = t0 // TB
        for mt in range(8):
            s = self.wnext("win_a", mt)
            if self.dry:
                continue
            for (c, c0, cn) in self.CH:
                pt, pb = self.ps()
                kb.group("pe", [self.h_b[kt][c] for kt in range(DT)] + [self.ring_b[s]], [pb],
                         [("matmul", dict(out=pt, lhsT=self.wt(s, kt), rhs=self.h[:, kt, c0:c0 + cn],
                                          start=(kt == 0), stop=(kt == DT - 1))) for kt in range(DT)])
                if mt < 4:
                    kb.op("act", [pb], [self.act_b[8 + mt][c]], "copy", out=self.arow(8 + mt, c * 512, 512), in_=pt)
                else:
                    kb.op("act", [pb], [self.act_b[mt - 4][c]], "activation",
                          out=self.arow(mt - 4, c * 512, 512), in_=pt, func=AF.Gelu_apprx_tanh)
        if not self.dry:
            kb.dma([b for r in range(8, 12) for b in self.act_b[r]], [],
                   out=self.Dzn.ap().rearrange("ct p t -> p ct t")[:, :, t0:t0 + NT],
                   in_=self.av(BF16, OFF_ACT + 8 * NT * 2, [[NT, 4], [1, NT]]))
        sl = [self.wnext("win_v", q) for q in range(4)]
        if self.dry:
            return
        NCH = NT // 128
        vgs = [self.arow_f32(12 + j8, 0, 512) for j8 in range(NCH)]
        stb = [self.lnst_b[0]]
        for j8 in range(NCH):
            c = j8 // 4
            pt, pb = self.ps()
            t_lo = HALO + 128 * j8
            kb.group("pe", [self.h_b[kt][c] for kt in range(DT)] + [self.ring_b[s] for s in sl], [pb],
                     [("matmul", dict(out=pt, lhsT=self.h[:, kt, t_lo:t_lo + 128],
                                      rhs=self.ring[:, sl[kt // 2], (kt % 2) * 512:(kt % 2) * 512 + 512],
                                      start=(kt == 0), stop=(kt == DT - 1))) for kt in range(DT)])
            vg, vgb = vgs[j8]
            st = self.lnst[:, j8, :]
            kb.op("act", [pb], vgb, "activation", out=vg, in_=pt, func=AF.Gelu_apprx_tanh)
            kb.op("dve", vgb, stb, "bn_stats", out=st[:, 0:6], in_=vg)
            kb.op("dve", stb, stb, "bn_aggr", out=st[:, 6:8], in_=st[:, 0:6])
        kb.op("act", stb + [self.c_b], stb, "activation", out=self.lnst[:, :, 8], in_=self.lnst[:, :, 7], func=AF.Sqrt,
              bias=self.epsc[:], scale=1.0)
        kb.op("dve", stb, stb, "reciprocal", out=self.lnst[:, :, 8], in_=self.lnst[:, :, 8])
        for j8 in range(NCH):
            c = j8 // 4
            k = j8 % 2
            vg, vgb = vgs[j8]
            st = self.lnst[:, j8, :]
            kb.op("dve", vgb + stb, vgb, "tensor_scalar", out=vg, in0=vg, scalar1=st[:, 6:7], scalar2=st[:, 8:9],
                  op0=ALU.subtract, op1=ALU.mult)
            kb.op("pool", vgb + [self.c_b], vgb, "tensor_tensor", out=vg, in0=vg, in1=self.gbt[:, 0, :], op=ALU.mult)
            vln = self.arow(20, k * 512, 512)
            vlb = [self.act_b[20][k]]
            kb.op("pool", vgb + [self.c_b], vlb, "tensor_tensor", out=vln, in0=vg, in1=self.gbt[:, 1, :], op=ALU.add)
            pt2, pb2 = self.ps()
            calls = []
            for hd in range(4):
                calls.append(("matmul", dict(out=pt2[:, hd * 128:(hd + 1) * 128], lhsT=vln[:, hd * 128:(hd + 1) * 128],
                                             rhs=self.wst[:, hd * 128:(hd + 1) * 128], start=True, stop=False)))
                calls.append(("matmul", dict(out=pt2[:, hd * 128:(hd + 1) * 128], lhsT=self.ones[0:1, :],
                                             rhs=self.bsr[0:1, hd * 128:(hd + 1) * 128], start=False, stop=True)))
            kb.group("pe", vlb + [self.c_b], [pb2], calls)
            ybo = self.av(BF16, OFF_ACT + (4 * NT + 128 * j8) * 2, [[NT, 4], [1, 128]])
            ubi = self.av(BF16, OFF_ACT + (128 * j8) * 2, [[NT, 4], [1, 128]])
            kb.op("dve", [pb2] + [self.act_b[r][c] for r in range(4)], [self.act_b[4 + r][c] for r in range(4)],
                  "tensor_tensor", out=ybo, in0=pt2.rearrange("p (a b) -> p a b", a=4), in1=ubi, op=ALU.mult)
        self.dyb_b[t0] = Buf()
        kb.dma([b for r in range(4, 8) for b in self.act_b[r]], [self.dyb_b[t0]],
               out=self.Dyb.ap().rearrange("ct p t -> p ct t")[:, :, t0:t0 + NT],
               in_=self.av(BF16, OFF_ACT + 4 * NT * 2, [[NT, 4], [1, NT]]))

    def mixC(self, t0):
        kb = self.kb
        col0 = t0 // TB
        if not self.dry:
            kb.dma([], [b for r in range(0, 4) for b in self.act_b[r]],
                   out=self.av(BF16, OFF_ACT, [[NT, 4], [1, NT]]),
                   in_=self.Dgn.ap().rearrange("ct p t -> p ct t")[:, :, t0:t0 + NT])
            kb.dma([self.dyb_b[t0]], [b for r in range(8, 12) for b in self.act_b[r]],
                   out=self.av(BF16, OFF_ACT + 8 * NT * 2, [[NT, 4], [1, NT]]),
                   in_=self.Dyb.ap().rearrange("ct p t -> p ct t")[:, :, t0:t0 + NT])
        for u in range(2):
            s = self.wnext("wglu", u)
            if self.dry:
                continue
            for ml in range(2):
                mt = 2 * u + ml
                for c in range(2):
                    pt, pb = self.ps()
                    kb.group("pe", [self.act_b[kt][c] for kt in range(4)] + [self.ring_b[s]], [pb],
                             [("matmul", dict(out=pt, lhsT=self.wt(s, ml * 4 + kt), rhs=self.arow(kt, c * 512, 512),
                                              start=(kt == 0), stop=(kt == 3))) for kt in range(4)])
                    si = self.sgi % 4
                    self.sgi += 1
                    kb.op("act", [pb], [self.sg_b[si]], "activation", out=self.sg[:, si, :], in_=pt, func=AF.Sigmoid)
                    kb.op("dve", [self.act_b[mt][c], self.sg_b[si]], [self.act_b[4 + mt][c]], "tensor_tensor",
                          out=self.arow(4 + mt, c * 512, 512), in0=self.arow(mt, c * 512, 512), in1=self.sg[:, si, :],
                          op=ALU.mult)
        for dt in range(DT):
            s = self.wnext("wout", dt)
            if self.dry:
                continue
            for (c, c0, cn) in self.CH:
                pt, pb = self.ps()
                rd = [self.act_b[4 + r][c] for r in range(8)]
                kb.group("pe", rd + [self.ring_b[s]], [pb],
                         [("matmul", dict(out=pt, lhsT=self.wt(s, kt), rhs=self.arow(4 + kt, c * 512, 512),
                                          start=(kt == 0), stop=(kt == DT - 1))) for kt in range(DT)])
                kb.op("dve", [pb, self.x_b[dt][c]], [self.x_b[dt][c]], "tensor_tensor",
                      out=self.x[:, dt, c0:c0 + cn], in0=pt, in1=self.x[:, dt, c0:c0 + cn], op=ALU.add)

    def small_setup(self):
        kb = self.kb
        tmp = self.av(F32, 0, [[1, 2048]])
        tb = Buf()
        kb.dma([], [tb], out=tmp, in_=self.sgup_d.ap())
        kb.op("dve", [tb], [self.c_b], "tensor_copy", out=self.wst[:], in_=tmp[:, 0:512])
        kb.op("dve", [tb], [self.c_b], "tensor_copy", out=self.gbt[:, 0, :], in_=tmp[:, 512:1024])
        kb.op("dve", [tb], [self.c_b], "tensor_copy", out=self.gbt[:, 1, :], in_=tmp[:, 1024:1536])
        kb.op("dve", [tb], [self.c_b], "tensor_copy", out=self.bsr[0:1, :], in_=tmp[0:1, 1536:2048])
        kb.barrier()

    def s5_setup(self):
        kb = self.kb
        C = S5C
        off = [0]

        def T(n, esz=4):
            o = off[0]
            off[0] += ((n * esz + 3) // 4) * 4
            return o

        def f(o, dims):
            return self.av(F32, o, dims)

        NK = 2 * TB + 1
        o_p5 = T(C["N"])
        b_p5 = Buf()
        kb.dma([], [b_p5], out=f(o_p5, [[1, C["N"]]]), in_=self.s5p_d.ap())

        def p5(name, dims):
            return f(o_p5 + 4 * C[name], dims)
        o_kvi, o_kv, o_one, o_id = T(NK), T(NK), T(256), T(128)
        o_mask = T(4 * 256)
        b_c = Buf()
        kvi = bass.AP(self.ar_bf.bitcast(I32), o_kvi // 4, [[ARENA // 4, 128], [1, NK]])
        kb.op("pool", [], [b_c], "iota", out=kvi, pattern=[[1, NK]], base=-TB, channel_multiplier=0)
        kb.op("dve", [b_c], [b_c], "tensor_copy", out=f(o_kv, [[1, NK]]), in_=kvi)
        kb.op("pool", [], [b_c], "memset", ap=f(o_one, [[1, 256]]), constant=1.0)
        kb.op("pool", [b_c], [b_c], "affine_select", out=f(o_id, [[1, 128]]), in_=f(o_one, [[1, 128]]),
              pattern=[[1, 128]], compare_op=ALU.is_equal, fill=0.0, base=0, channel_multiplier=-1)
        for ks in range(2):
            kb.op("pool", [b_c], [b_c], "affine_select", out=f(o_mask + ks * 1024, [[16, 16], [1, 16]]),
                  in_=f(o_one, [[16, 16], [1, 16]]), pattern=[[16, 16], [0, 16]], compare_op=ALU.is_ge, fill=0.0,
                  base=15 - 128 * ks, channel_multiplier=-1)
            kb.op("pool", [b_c], [b_c], "affine_select", out=f(o_mask + 2048 + ks * 1024, [[16, 16], [1, 16]]),
                  in_=f(o_one, [[16, 16], [1, 16]]), pattern=[[-16, 16], [0, 16]], compare_op=ALU.is_ge, fill=0.0,
                  base=128 * ks, channel_multiplier=1)
        o_dt, o_a, o_th = T(32), T(32), T(32)
        b_s = Buf()
        kb.op("act", [b_p5], [b_s], "activation", out=f(o_dt, [[1, 32]]), in_=p5("LDT", [[1, 32]]), func=AF.Exp)
        kb.op("dve", [b_p5, b_s], [b_s], "tensor_tensor", out=f(o_a, [[1, 32]]), in0=p5("LR", [[1, 32]]),
              in1=f(o_dt, [[1, 32]]), op=ALU.mult)
        kb.op("dve", [b_p5, b_s], [b_s], "tensor_tensor", out=f(o_th, [[1, 32]]), in0=p5("LI", [[1, 32]]),
              in1=f(o_dt, [[1, 32]]), op=ALU.mult)
        NP_ = 32 * NK
        o_mag, o_ang, o_v, o_vi, o_vf, o_m, o_pwr, o_pwi = [T(NP_) for _ in range(8)]
        d3 = [[NK, 32], [1, NK]]
        b_t = Buf()
        kb.op("dve", [b_s, b_c], [b_t], "tensor_tensor", out=f(o_mag, d3), in0=f(o_a, [[1, 32], [0, NK]]),
              in1=f(o_kv, [[0, 32], [1, NK]]), op=ALU.mult)
        kb.op("act", [b_t], [b_t], "activation", out=f(o_mag, d3), in_=f(o_mag, d3), func=AF.Exp)
        kb.op("dve", [b_s, b_c], [b_t], "tensor_tensor", out=f(o_ang, d3), in0=f(o_th, [[1, 32], [0, NK]]),
              in1=f(o_kv, [[0, 32], [1, NK]]), op=ALU.mult)
        vi = bass.AP(self.ar_bf.bitcast(I32), o_vi // 4, [[ARENA // 4, 128]] + d3)
        for (phase, o_out) in ((0.25, o_pwr), (0.0, o_pwi)):
            kb.op("dve", [b_t], [b_t], "tensor_scalar", out=f(o_v, d3), in0=f(o_ang, d3),
                  scalar1=1.0 / (2 * math.pi), scalar2=256.0 + phase, op0=ALU.mult, op1=ALU.add)
            kb.op("dve", [b_t], [b_t], "tensor_copy", out=vi, in_=f(o_v, d3))
            kb.op("dve", [b_t], [b_t], "tensor_copy", out=f(o_vf, d3), in_=vi)
            kb.op("dve", [b_t], [b_t], "tensor_tensor", out=f(o_v, d3), in0=f(o_v, d3), in1=f(o_vf, d3), op=ALU.subtract)
            kb.op("dve", [b_t], [b_t], "tensor_single_scalar", out=f(o_m, d3), in_=f(o_v, d3), scalar=0.5, op=ALU.is_gt)
            kb.op("dve", [b_t], [b_t], "tensor_tensor", out=f(o_v, d3), in0=f(o_v, d3), in1=f(o_m, d3), op=ALU.subtract)
            kb.op("dve", [b_t], [b_t], "tensor_single_scalar", out=f(o_m, d3), in_=f(o_v, d3), scalar=-0.5, op=ALU.is_lt)
            kb.op("dve", [b_t], [b_t], "tensor_tensor", out=f(o_v, d3), in0=f(o_v, d3), in1=f(o_m, d3), op=ALU.add)
            kb.op("act", [b_t], [b_t], "activation", out=f(o_v, d3), in_=f(o_v, d3), func=AF.Sin, scale=6.28318)
            kb.op("dve", [b_t], [b_t], "tensor_tensor", out=f(o_out, d3), in0=f(o_v, d3), in1=f(o_mag, d3), op=ALU.mult)
        k1 = TB + 1
        o_nr, o_den, o_t1, o_t2, o_qr, o_qi = [T(32) for _ in range(6)]
        v32 = [[1, 32]]
        pw1r, pw1i = f(o_pwr + 4 * k1, [[NK, 32]]), f(o_pwi + 4 * k1, [[NK, 32]])
        LR, LI = p5("LR", v32), p5("LI", v32)
        b_q = Buf()
        kb.op("dve", [b_t], [b_q], "tensor_scalar", out=f(o_nr, v32), in0=pw1r, scalar1=-1.0, scalar2=None, op0=ALU.add)
        kb.op("dve", [b_p5], [b_q], "tensor_tensor", out=f(o_den, v32), in0=LR, in1=LR, op=ALU.mult)
        kb.op("dve", [b_p5], [b_q], "tensor_tensor", out=f(o_t1, v32), in0=LI, in1=LI, op=ALU.mult)
        kb.op("dve", [b_q], [b_q], "tensor_tensor", out=f(o_den, v32), in0=f(o_den, v32), in1=f(o_t1, v32), op=ALU.add)
        kb.op("dve", [b_q], [b_q], "reciprocal", out=f(o_den, v32), in_=f(o_den, v32))
        kb.op("dve", [b_q, b_p5], [b_q], "tensor_tensor", out=f(o_t1, v32), in0=f(o_nr, v32), in1=LR, op=ALU.mult)
        kb.op("dve", [b_q, b_p5, b_t], [b_q], "tensor_tensor", out=f(o_t2, v32), in0=pw1i, in1=LI, op=ALU.mult)
        kb.op("dve", [b_q], [b_q], "tensor_tensor", out=f(o_t1, v32), in0=f(o_t1, v32), in1=f(o_t2, v32), op=ALU.add)
        kb.op("dve", [b_q], [b_q], "tensor_tensor", out=f(o_qr, v32), in0=f(o_t1, v32), in1=f(o_den, v32), op=ALU.mult)
        kb.op("dve", [b_q, b_p5, b_t], [b_q], "tensor_tensor", out=f(o_t1, v32), in0=pw1i, in1=LR, op=ALU.mult)
        kb.op("dve", [b_q, b_p5], [b_q], "tensor_tensor", out=f(o_t2, v32), in0=f(o_nr, v32), in1=LI, op=ALU.mult)
        kb.op("dve", [b_q], [b_q], "tensor_tensor", out=f(o_t1, v32), in0=f(o_t1, v32), in1=f(o_t2, v32), op=ALU.subtract)
        kb.op("dve", [b_q], [b_q], "tensor_tensor", out=f(o_qi, v32), in0=f(o_t1, v32), in1=f(o_den, v32), op=ALU.mult)
        o_bbr, o_bbi, o_u1, o_u2 = T(512), T(512), T(512), T(512)
        dB = [[16, 32], [1, 16]]
        qrb, qib = f(o_qr, [[1, 32], [0, 16]]), f(o_qi, [[1, 32], [0, 16]])
        BR, BI = p5("BR", dB), p5("BI", dB)
        b_bb = Buf()
        kb.op("dve", [b_q, b_p5], [b_bb], "tensor_tensor", out=f(o_u1, dB), in0=qrb, in1=BR, op=ALU.mult)
        kb.op("dve", [b_q, b_p5], [b_bb], "tensor_tensor", out=f(o_u2, dB), in0=qib, in1=BI, op=ALU.mult)
        kb.op("dve", [b_bb], [b_bb], "tensor_tensor", out=f(o_bbr, dB), in0=f(o_u1, dB), in1=f(o_u2, dB), op=ALU.subtract)
        kb.op("dve", [b_q, b_p5, b_bb], [b_bb], "tensor_tensor", out=f(o_u1, dB), in0=qrb, in1=BI, op=ALU.mult)
        kb.op("dve", [b_q, b_p5, b_bb], [b_bb], "tensor_tensor", out=f(o_u2, dB), in0=qib, in1=BR, op=ALU.mult)
        kb.op("dve", [b_bb], [b_bb], "tensor_tensor", out=f(o_bbi, dB), in0=f(o_u1, dB), in1=f(o_u2, dB), op=ALU.add)
        for d in range(2):
            arv = f(o_pwr + 4 * (d * 16 * NK + 2 * TB), [[NK, 16]])
            aiv = f(o_pwi + 4 * (d * 16 * NK + 2 * TB), [[NK, 16]])
            kb.op("dve", [b_t], [self.c_b], "tensor_copy", out=self.lam16[:, d, 0, 0:16], in_=arv)
            kb.op("dve", [b_t], [self.c_b], "tensor_copy", out=self.lam16[:, d, 0, 16:32], in_=arv)
            kb.op("dve", [b_t], [self.c_b], "tensor_scalar", out=self.lam16[:, d, 1, 0:16], in0=aiv, scalar1=-1.0,
                  scalar2=None, op0=ALU.mult)
            kb.op("dve", [b_t], [self.c_b], "tensor_copy", out=self.lam16[:, d, 1, 16:32], in_=aiv)
        NE = 4 * TB * 16
        o_m1, o_m2, o_m3, o_m4 = [T(NE) for _ in range(4)]
        o_X = {nm: (T(NE), T(NE)) for nm in ("A", "P", "Q")}
        o_tmpM = T(4 * 2 * 2 * 256)
        o_tmp2 = T(256)
        o_pack = T(4 * 3072, 2)
        assert off[0] <= ARENA, off[0]
        dX = [[TB * 16, 4], [16, TB], [1, 16]]
        b_m = [Buf() for _ in range(4)]
        b_X = {nm: Buf() for nm in ("A", "P", "Q")}
        b_tmpM, b_tmp2, b_pack = Buf(), Buf(), Buf()
        pack = self.av(BF16, o_pack, [[3072, 4], [1, 3072]])
        ei = [0]

        def ve():
            ei[0] += 1
            return "pool" if ei[0] % 4 == 0 else "dve"
        for ch in range(4):
            for d in range(2):
                dg0 = d * 16 + 4 * ch
                pat = {"A": (2 * TB - 1, -1) if d == 0 else (TB, 1),
                       "P": (TB - 1, -1) if d == 0 else (0, 1),
                       "Q": (TB + 1, 1) if d == 0 else (2 * TB, -1)}
                for nm in ("A", "P", "Q"):
                    st_, sp_ = pat[nm]
                    pwr = f(o_pwr + 4 * (dg0 * NK + st_), [[NK, 4], [sp_, TB], [0, 16]])
                    pwi = f(o_pwi + 4 * (dg0 * NK + st_), [[NK, 4], [sp_, TB], [0, 16]])
                    if nm == "Q":
                        xr = f(o_p5 + 4 * (C["CR"] + dg0 * 16), [[16, 4], [0, TB], [1, 16]])
                        xi = f(o_p5 + 4 * (C["CI"] + dg0 * 16), [[16, 4], [0, TB], [1, 16]])
                        xb = [b_p5]
                    else:
                        xr = f(o_bbr + 4 * dg0 * 16, [[16, 4], [0, TB], [1, 16]])
                        xi = f(o_bbi + 4 * dg0 * 16, [[16, 4], [0, TB], [1, 16]])
                        xb = [b_bb]
                    (orr, oii) = o_X[nm]
                    e1, e2 = ve(), ve()
                    kb.op(e1, [b_t] + xb, [b_m[0]], "tensor_tensor", out=f(o_m1, dX), in0=pwr, in1=xr, op=ALU.mult)
                    kb.op(e1, [b_t] + xb, [b_m[1]], "tensor_tensor", out=f(o_m2, dX), in0=pwi, in1=xi, op=ALU.mult)
                    kb.op(e1, [b_m[0], b_m[1]], [b_X[nm]], "tensor_tensor", out=f(orr, dX), in0=f(o_m1, dX),
                          in1=f(o_m2, dX), op=ALU.subtract)
                    kb.op(e2, [b_t] + xb, [b_m[2]], "tensor_tensor", out=f(o_m3, dX), in0=pwr, in1=xi, op=ALU.mult)
                    kb.op(e2, [b_t] + xb, [b_m[3]], "tensor_tensor", out=f(o_m4, dX), in0=pwi, in1=xr, op=ALU.mult)
                    kb.op(e2, [b_m[2], b_m[3]], [b_X[nm]], "tensor_tensor", out=f(oii, dX), in0=f(o_m3, dX),
                          in1=f(o_m4, dX), op=ALU.add)
                    if nm == "Q":
                        kb.op("dve", [b_X[nm]], [b_X[nm]], "tensor_scalar", out=f(oii, dX), in0=f(oii, dX),
                              scalar1=-1.0, scalar2=None, op0=ALU.mult)
                for ri in range(2):
                    kb.op("act", [b_X["Q"]], [b_pack], "copy",
                          out=self.av(BF16, o_pack + 2 * (1024 + (d * 2 + ri) * 256), [[3072, 4], [1, 256]]),
                          in_=f(o_X["Q"][ri], [[256, 4], [1, 256]]))
                for gi in range(4):
                    for gpar in range(2):
                        g = 2 * (4 * ch + gi) + gpar
                        ps_ = slice(gpar * 64, gpar * 64 + 64)
                        self.bg_step("act")
                        for ks in range(2):
                            Ar = f(o_X["A"][0] + 4 * (gi * 256 + ks * 128), [[1, 128]])[ps_, :]
                            Ai = f(o_X["A"][1] + 4 * (gi * 256 + ks * 128), [[1, 128]])[ps_, :]
                            Pr = f(o_X["P"][0] + 4 * (gi * 256 + ks * 128), [[1, 128]])[ps_, :]
                            Pi = f(o_X["P"][1] + 4 * (gi * 256 + ks * 128), [[1, 128]])[ps_, :]
                            Qr = f(o_X["Q"][0] + 4 * (gi * 256), [[1, 256]])[ps_, :]
                            Qi = f(o_X["Q"][1] + 4 * (gi * 256), [[1, 256]])[ps_, :]
                            idn = f(o_id, [[1, 128]])[ps_, gpar * 64:gpar * 64 + 64]
                            pt, pb = self.ps()
                            kb.group("pe", [b_X["A"], b_c], [pb],
                                     [("matmul", dict(out=pt[:, 0:64], lhsT=Ar, rhs=idn, start=True, stop=True)),
                                      ("matmul", dict(out=pt[:, 64:128], lhsT=Ai, rhs=idn, start=True, stop=True))])
                            kb.op("act", [pb], [b_pack], "copy",
                                  out=self.av(BF16, o_pack + 2 * (gi * 3072 + ((gpar * 2 + d) * 2 + ks) * 128), [[1, 128]]),
                                  in_=pt[:, 0:128])
                            pt, pb = self.ps()
                            kb.group("pe", [b_X["P"], b_X["Q"]], [pb],
                                     [("matmul", dict(out=pt[:, 0:256], lhsT=Pr, rhs=Qr, start=True, stop=False)),
                                      ("matmul", dict(out=pt[:, 0:256], lhsT=Pi, rhs=Qi, start=False, stop=True))])
                            tm = f(o_tmpM + 4 * (((gi * 2 + gpar) * 2 + ks) * 256), [[1, 256]])
                            if d == 0:
                                kb.op("dve", [pb, b_c], [b_tmpM], "tensor_tensor", out=tm, in0=pt[:, 0:256],
                                      in1=f(o_mask + ks * 1024, [[1, 256]]), op=ALU.mult)
                            else:
                                t2 = f(o_tmp2, [[1, 256]])
                                kb.op("dve", [pb, b_c], [b_tmp2], "tensor_tensor", out=t2, in0=pt[:, 0:256],
                                      in1=f(o_mask + 2048 + ks * 1024, [[1, 256]]), op=ALU.mult)
                                kb.op("dve", [b_tmp2, b_tmpM], [b_tmpM], "tensor_tensor", out=tm, in0=tm, in1=t2, op=ALU.add)
                                blk = tm[:, ks * 128:ks * 128 + 128]
                                kb.op("dve", [b_tmpM, b_c, b_p5], [b_tmpM], "scalar_tensor_tensor", out=blk,
                                      in0=f(o_id, [[1, 128]]), scalar=f(o_p5 + 4 * (C["DL"] + g), [[1, 1]]), in1=blk,
                                      op0=ALU.mult, op1=ALU.add)
                                kb.op("act", [b_tmpM], [b_pack], "copy",
                                      out=self.av(BF16, o_pack + 2 * (gi * 3072 + 2048 + (gpar * 2 + ks) * 256), [[1, 256]]),
                                      in_=tm)
            kb.dma([b_pack], self.smat_b[4 * ch:4 * ch + 4],
                   out=self.Smat.ap().rearrange("g p n -> p g n")[:, 4 * ch:4 * ch + 4, :], in_=pack)
        kb.barrier()

    def s5_scan(self):
        if self.dry:
            return
        kb = self.kb
        kb.barrier()
        O_U, O_SH, O_MAT, O_GST, O_GIL, O_GNAT = 0, 49152, 98304, 110592, 122880, 135168
        Uv = self.av(BF16, O_U, [[32 * NB, 2], [NB, 32], [1, NB]])
        SH = self.av(BF16, O_SH, [[2 * 16 * NB, 2], [16 * NB, 2], [NB, 16], [1, NB]])
        mats = [self.av(BF16, O_MAT + k * 6144, [[1, 3072]]) for k in range(2)]
        gsts = [self.av(BF16, O_GST, [[8 * NB, 2], [NB, 8], [1, NB]]),
                bass.AP(self.bgst.bitcast(BF16), 0, [[8192, 128], [8 * NB, 2], [NB, 8], [1, NB]])]
        u_bs, mat_b, gst_bs = [Buf() for _ in range(TB)], [Buf(), Buf()], [Buf(), Buf()]
        sh_s, sh_h = [Buf(), Buf()], [Buf(), Buf()]
        nats = [self.av(BF16, O_GST, [[1, LT]]), self.av(BF16, O_GNAT, [[1, LT]])]
        blks = [self.av(BF16, O_GIL, [[NB, TB], [1, NB]]), self.av(BF16, O_MAT, [[NB, TB], [1, NB]])]
        nat_b, blk_b, dz_b = [Buf(), Buf()], [Buf(), Buf()], Buf()
        dg_b, gil_b, gn_b = Buf(), Buf(), Buf()
        def b0_in(ct):
            kb.dma([], [nat_b[ct % 2]], out=nats[ct % 2], in_=self.Dzn[ct])

        def b0_rest(ct):
            k = ct % 2
            kb.op(["dve", "act"][k], [nat_b[k]], [blk_b[k]], "tensor_copy" if k == 0 else "copy",
                  out=blks[k], in_=nats[k].rearrange("p (j s) -> p s j", s=TB))
            kb.dma([blk_b[k]], [dz_b], out=self.Dz[ct], in_=blks[k])
        b0_in(0)
        b0_in(1)
        for ct in range(4):
            b0_rest(ct)
            if ct + 2 < 4:
                b0_in(ct + 2)
        for s in range(TB):
            kb.dma([dz_b], [u_bs[s]], out=Uv[(s % 8) * 16:(s % 8) * 16 + 16, s // 8, :, :],
                   in_=self.Dz.ap()[:, :, s, :].rearrange("ct (gl c) j -> c (ct gl) j", c=16))
        def load_mats(gp, extra=()):
            kb.dma([self.smat_b[gp]], [mat_b[gp % 2]] + list(extra), out=mats[gp % 2], in_=self.Smat[gp])
        load_mats(0, [blk_b[1]])
        for gp in range(16):
            m = mats[gp % 2]
            if gp + 1 < 16:
                load_mats(gp + 1, [blk_b[1]] if gp == 0 else [])
            for d in range(2):
                for ri in range(2):
                    pt, pb = self.ps()
                    calls = []
                    for gpar in range(2):
                        for ks in range(2):
                            c0 = ((gpar * 2 + d) * 2 + ks) * 128 + ri * 64
                            calls.append(("matmul", dict(out=pt[gpar * 64:gpar * 64 + 64, 0:NB], lhsT=m[:, c0:c0 + 64],
                                                         rhs=Uv[:, ks, 2 * gp + gpar, :], start=(ks == 0), stop=(ks == 1))))
                    kb.group("pe", u_bs + [mat_b[gp % 2]], [pb], calls)
                    kb.op("act", [pb], [sh_s[d]], "copy", out=SH[:, d, ri, gp, :], in_=pt[:, 0:NB])
        load_mats(0)
        load_mats(1)

        def st(o, dims=None):
            return bass.AP(self.stt, o, [[512, 128]] + (dims or [[1, 32]]))
        chains = []
        for ci, (lo, hi, en) in enumerate(((0, NBP, "dve"), (NBP, NB, "pool"))):
            base = ci * 256
            v3 = [[32, 2], [16, 2], [1, 16]]
            chains.append(dict(lo=lo, hi=hi, en=en,
                               X=[st(base + 64 * k, v3) for k in range(2)],
                               Xs=[st(base + 64 * k + 16, [[32, 2], [-16, 2], [1, 16]]) for k in range(2)],
                               T1=st(base + 128, v3), T2=st(base + 192, v3), xb=[Buf(), Buf()], tb=[Buf(), Buf()]))
        A1 = self.lam16[:, :, 0, :].rearrange("p d (a b) -> p d a b", a=2)
        A2 = self.lam16[:, :, 1, :].rearrange("p d (a b) -> p d a b", a=2)
        for k in range(NBP):
            for ch in chains:
                lo, hi, en = ch["lo"], ch["hi"], ch["en"]
                if k >= hi - lo:
                    continue
                jf, jb = lo + k, hi - 1 - k
                Sj = self.av(BF16, O_SH + 2 * jf, [[2 * 16 * NB + jb - jf, 2], [16 * NB, 2], [NB, 16]])
                xp, xn = k % 2, (k + 1) % 2
                X, Xs, T1, T2, xb, tb = ch["X"], ch["Xs"], ch["T1"], ch["T2"], ch["xb"], ch["tb"]
                if k == 0:
                    kb.op(en, sh_s, [xb[xn]], "tensor_copy", out=X[xn], in_=Sj)
                    continue
                kb.op(en, [xb[xp], self.c_b], [tb[0]], "tensor_tensor", out=T1, in0=A1, in1=X[xp], op=ALU.mult)
                kb.op(en, [xb[xp], self.c_b], [tb[1]], "tensor_tensor", out=T2, in0=A2, in1=Xs[xp], op=ALU.mult)
                kb.op(en, [tb[0], tb[1]], [tb[0]], "tensor_tensor", out=T1, in0=T1, in1=T2, op=ALU.add)
                kb.op(en, [tb[0]] + sh_s, [xb[xn]], "tensor_tensor", out=X[xn], in0=T1, in1=Sj, op=ALU.add)
                kb.op("act", [xb[xn]], sh_h, "copy", out=Sj, in_=X[xn])
        gil = self.av(BF16, O_GIL, [[NB, TB], [1, NB]])
        gnat = self.av(BF16, O_GNAT, [[1, LT]])
        deferred, nxt = [], []
        dg_bs = [Buf() for _ in range(16)]
        pending_mats = None
        for gp in range(16):
            m = mats[gp % 2]
            if pending_mats is not None:
                load_mats(pending_mats)
            pending_mats = gp + 2 if gp + 2 < 16 else None
            for gpar in range(2):
                g = 2 * gp + gpar
                gl, ct = g % 8, g // 8
                gst, gst_b = gsts[ct % 2], gst_bs[ct % 2]
                ps_ = slice(gpar * 64, gpar * 64 + 64)
                for mt in range(2):
                    pt, pb = self.ps()
                    calls = []
                    for ks in range(2):
                        c0 = 2048 + (gpar * 2 + ks) * 256 + mt * 128
                        calls.append(("matmul", dict(out=pt[:, 0:NB], lhsT=m[:, c0:c0 + 128], rhs=Uv[:, ks, g, :],
                                                     start=(ks == 0), stop=False)))
                    for ri in range(2):
                        c0 = 1024 + (0 * 2 + ri) * 256 + mt * 128
                        for (a, b) in ((1, NBP), (NBP + 1, NB)):
                            calls.append(("matmul", dict(out=pt[:, a:b], lhsT=m[ps_, c0:c0 + 128],
                                                         rhs=SH[ps_, 0, ri, gp, a - 1:b - 1], start=False, stop=False)))
                    for ri in range(2):
                        c0 = 1024 + (1 * 2 + ri) * 256 + mt * 128
                        for (a, b) in ((0, NBP - 1), (NBP, NB - 1)):
                            last = (ri == 1 and a == NBP)
                            calls.append(("matmul", dict(out=pt[:, a:b], lhsT=m[ps_, c0:c0 + 128],
                                                         rhs=SH[ps_, 1, ri, gp, a + 1:b + 1], start=False, stop=last)))
                    kb.group("pe", u_bs + [mat_b[gp % 2], sh_h[0], sh_h[1], sh_s[0], sh_s[1]], [pb], calls)
                    kb.op("act", [pb], [gst_b, nat_b[0]], "activation", out=gst[:, mt, gl, :], in_=pt[:, 0:NB],
                          func=AF.Gelu_apprx_tanh)
                if gl == 7:
                    for fn in deferred:
                        fn()
                    deferred = nxt
                    nxt = []
                    for mt in range(2):
                        for t8 in range(8):
                            kb.dma([gst_b], [dg_bs[8 * mt + t8]],
                                   out=self.Dg[ct][8 * mt + t8].rearrange("(gl c) j -> c gl j", c=16),
                                   in_=gst[t8 * 16:t8 * 16 + 16, mt, :, :])

                    def stage2(ct=ct):
                        kb.dma(dg_bs, [gil_b, blk_b[0]], out=gil, in_=self.Dg[ct].rearrange("t p j -> p t j"))
                        kb.op("dve", [gil_b], [gn_b, nat_b[1]], "tensor_copy",
                              out=gnat.rearrange("p (j s) -> p s j", s=TB), in_=gil)
                        nxt.append(lambda ct=ct: kb.dma([gn_b], [], out=self.Dgn[ct], in_=gnat))
                    deferred.append(stage2)
        while deferred or nxt:
            for fn in deferred:
                fn()
            deferred = nxt
            nxt = []
        kb.barrier()
        self.wissued = self.wi

    def precast_init(self):
        NU = self.NU
        self.bg_q = [(p0, min(2, NU - p0)) for p0 in range(0, NU, 2)]
        self.bg_i = 0
        self.bg_in_b = [Buf(), Buf(), Buf()]
        self.bg_out_b = [Buf(), Buf()]

    def bg_step(self, en="act"):
        if self.dry or self.bg_i >= len(self.bg_q) + 2:
            return
        kb = self.kb
        i = self.bg_i
        self.bg_i += 1
        q = self.bg_q
        ktf = self.ktb.bitcast(F32)
        ins = [self.bgst[:, 0:2048], self.bgst[:, 2048:4096], bass.AP(ktf, 0, [[2048, 128], [1, 2048]])]
        vv2 = bass.AP(self.vvb, 0, [[4096, 128], [2048, 2], [1, 2048]])
        if i < len(q):
            p0, n = q[i]
            kb.dma([], [self.bg_in_b[i % 3]], out=ins[i % 3][:, 0:n * 1024], in_=self.wall[:, p0 * 1024:(p0 + n) * 1024])
        if 1 <= i <= len(q):
            p0, n = q[i - 1]
            kb.op(en, [self.bg_in_b[(i - 1) % 3]], [self.bg_out_b[(i - 1) % 2]], "copy" if en == "act" else "tensor_copy",
                  out=vv2[:, (i - 1) % 2, 0:n * 1024], in_=ins[(i - 1) % 3][:, 0:n * 1024])
        if 2 <= i <= len(q) + 1:
            p0, n = q[i - 2]
            kb.dma([self.bg_out_b[(i - 2) % 2]], [self.wbf_b[p0 + k] for k in range(n)],
                   out=self.wbf[:, p0 * 1024:(p0 + n) * 1024], in_=vv2[:, (i - 2) % 2, 0:n * 1024])

    def bg_flush(self):
        if self.dry:
            return
        while self.bg_i < len(self.bg_q) + 2:
            self.bg_step()

    def load_x(self, src, src_b, t0):
        if self.dry:
            return
        kb = self.kb
        srcv = src.ap().rearrange("(dt p) t -> p dt t", p=128)
        wr = [self.x_b[dt][c] for dt in range(DT) for c in range(2)]
        rd = [] if src_b is None else list({id(src_b[(t0, dt)]): src_b[(t0, dt)] for dt in range(DT)}.values())
        kb.dma(rd, wr, out=self.x[:, :, HALO:HALO + NT], in_=srcv[:, :, t0:t0 + NT])

    def store_x(self, dst, dst_b, t0):
        if self.dry:
            return
        kb = self.kb
        dstv = dst.ap().rearrange("(dt p) t -> p dt t", p=128)
        rd = [self.x_b[dt][c] for dt in range(DT) for c in range(2)]
        wr = []
        if dst_b is not None:
            b = Buf()
            for dt in range(DT):
                dst_b[(t0, dt)] = b
            wr = [b]
        kb.dma(rd, wr, out=dstv[:, :, t0:t0 + NT], in_=self.x[:, :, HALO:HALO + NT])

    def load_x_dt(self, src, src_b, t0, dt):
        if self.dry:
            return
        rd = [] if src_b is None else [src_b[(t0, dt)]]
        self.kb.dma(rd, self.x_b[dt][0:2], out=self.x[:, dt, HALO:HALO + NT], in_=src[dt * 128:(dt + 1) * 128, t0:t0 + NT])

    def store_x_dt(self, dst, dst_b, t0, dt):
        if self.dry:
            return
        dst_b[(t0, dt)] = Buf()
        self.kb.dma(self.x_b[dt][0:2], [dst_b[(t0, dt)]], out=dst[dt * 128:(dt + 1) * 128, t0:t0 + NT],
                    in_=self.x[:, dt, HALO:HALO + NT])

    def calc_rstd(self, ranges=None, lnexp=False):
        kb = self.kb
        allx = [b for r in self.x_b for b in r]
        for (c0, cn) in (ranges or [(HALO, 512), (HALO + 512, 512)]):
            pt, pb = self.ps()
            calls = []
            for dt in range(DT):
                kb.op("act", allx, [self.sq_b[dt]], "activation",
                      out=self.sq[:, dt, 0:cn], in_=self.x[:, dt, c0:c0 + cn], func=AF.Square)
                calls.append(("matmul", dict(out=pt[:, 0:cn], lhsT=self.ones[:], rhs=self.sq[:, dt, 0:cn],
                                             start=(dt == 0), stop=(dt == DT - 1))))
            kb.group("pe", self.sq_b + [self.c_b], [pb], calls)
            if lnexp:
                kb.op("act", [pb, self.c_b], self.rs_b, "activation",
                      out=self.rs[:, c0:c0 + cn], in_=pt[:, 0:cn], func=AF.Ln, bias=self.epsc[:], scale=1.0 / D)
                kb.op("act", self.rs_b, self.rs_b, "activation",
                      out=self.rs[:, c0:c0 + cn], in_=self.rs[:, c0:c0 + cn], func=AF.Exp, scale=-0.5)
                continue
            kb.op("act", [pb, self.c_b], self.rs_b, "activation",
                  out=self.rs[:, c0:c0 + cn], in_=pt[:, 0:cn], func=AF.Sqrt,
                  bias=self.epsc[:], scale=1.0 / D)
            kb.op("dve", self.rs_b, self.rs_b, "reciprocal",
                  out=self.rs[:, c0:c0 + cn], in_=self.rs[:, c0:c0 + cn])

    def rmsnorm(self, vname, to_x=False, lnexp=False, to_act=False):
        if self.dry:
            return
        kb = self.kb
        self.calc_rstd(lnexp=lnexp)
        for (c, c0, cn) in self.CH:
            for dt in range(DT):
                if to_act:
                    dst_ap = self.av(F32, OFF_ACT + (2 * dt + c) * NT * 2, [[1, 512]])
                    dst_b = self.act_b[2 * dt + c]
                elif to_x:
                    dst_ap, dst_b = self.x[:, dt, c0:c0 + cn], [self.x_b[dt][c]]
                else:
                    dst_ap, dst_b = self.h[:, dt, c0:c0 + cn], [self.h_b[dt][c]]
                kb.op("dve", [self.x_b[dt][c], self.c_b] + self.rs_b, dst_b,
                      "scalar_tensor_tensor",
                      out=dst_ap, in0=self.x[:, dt, c0:c0 + cn],
                      scalar=self.vcol(vname, dt), in1=self.rs[:, c0:c0 + cn],
                      op0=ALU.mult, op1=ALU.mult)

    def ffn(self, which, layer, after_d=None):
        kb = self.kb
        self.rmsnorm(f"{which}_norm{layer}")
        for f in range(32):
            sg_ = self.wnext(f"{which}_gate{layer}", f)
            su_ = self.wnext(f"{which}_up{layer}", f)
            if self.dry:
                continue
            if self.bg_on:
                self.bg_step()
            for (c, c0, cn) in self.CH:
                pg, pgb = self.ps()
                pu, pub = self.ps()
                hb = [self.h_b[kt][c] for kt in range(DT)]
                kb.group("pe", hb + [self.ring_b[sg_]], [pgb],
                         [("matmul", dict(out=pg, lhsT=self.wt(sg_, kt), rhs=self.h[:, kt, c0:c0 + cn],
                                          start=(kt == 0), stop=(kt == DT - 1))) for kt in range(DT)])
                kb.group("pe", hb + [self.ring_b[su_]], [pub],
                         [("matmul", dict(out=pu, lhsT=self.wt(su_, kt), rhs=self.h[:, kt, c0:c0 + cn],
                                          start=(kt == 0), stop=(kt == DT - 1))) for kt in range(DT)])
                si = self.sgi % 4
                self.sgi += 1
                kb.op("act", [pgb], [self.sg_b[si]], "activation",
                      out=self.sg[:, si, :], in_=pg, func=AF.Silu)
                kb.op("dve", [pub, self.sg_b[si]], [self.act_b[f][c]], "tensor_tensor",
                      out=self.act[:, f, c * 512:(c + 1) * 512], in0=pu, in1=self.sg[:, si, :], op=ALU.mult)
        for d in range(DT):
            sl = [self.wnext(f"{which}_down{layer}", d * 4 + kb4) for kb4 in range(4)]
            if self.dry:
                continue
            for (c, c0, cn) in self.CH:
                pd, pdb = self.ps()
                kb.group("pe", [self.act_b[f][c] for f in range(32)] + [self.ring_b[s] for s in sl], [pdb],
                         [("matmul", dict(out=pd, lhsT=self.wt(sl[f // 8], f % 8),
                                          rhs=self.act[:, f, c * 512:(c + 1) * 512],
                                          start=(f == 0), stop=(f == 31))) for f in range(32)])
                kb.op("dve", [pdb, self.x_b[d][c]], [self.x_b[d][c]], "scalar_tensor_tensor",
                      out=self.x[:, d, c0:c0 + cn], in0=pd, scalar=0.5, in1=self.x[:, d, c0:c0 + cn],
                      op0=ALU.mult, op1=ALU.add)
            if after_d is not None:
                after_d(d)


    def arow(self, r, c0=0, cn=NT):
        return self.act[:, r, c0:c0 + cn]

    def arow_f32(self, r, off, n):
        ap = self.av(F32, OFF_ACT + (r * (NT // 2) + off) * 4, [[1, n]])
        r1 = (r * (NT // 2) + off + n - 1) // (NT // 2)
        bufs = [b for rr in range(r, r1 + 1) for b in self.act_b[rr]]
        return ap, bufs

    def kv_setup(self, layer, seq):
        kb = self.kb
        if not self.dry:
            mf, mfb = self.arow_f32(0, 0, DT * NMEM)
            mfv = mf.rearrange("p (a b) -> p a b", a=DT)
            hm = self.act[:, 4:6, :].rearrange("p r (a b) -> p (r a) b", a=4)
            hmb = self.act_b[4] + self.act_b[5]
            srcv = self.memT.ap().rearrange("(dt p) m -> p dt m", p=128)
            kb.dma([], mfb, out=mfv, in_=srcv[:, :, seq * NMEM:(seq + 1) * NMEM])
            pt, pb = self.ps()
            calls = []
            for dt in range(DT):
                kb.op("act", mfb, [self.sq_b[dt]], "activation", out=self.sq[:, dt, 0:NMEM], in_=mfv[:, dt, :],
                      func=AF.Square)
                calls.append(("matmul", dict(out=pt[:, 0:NMEM], lhsT=self.ones[:], rhs=self.sq[:, dt, 0:NMEM],
                                             start=(dt == 0), stop=(dt == DT - 1))))
            kb.group("pe", self.sq_b + [self.c_b], [pb], calls)
            kb.op("act", [pb, self.c_b], self.rs_b, "activation", out=self.rs[:, 0:NMEM], in_=pt[:, 0:NMEM],
                  func=AF.Sqrt, bias=self.epsc[:], scale=1.0 / D)
            kb.op("dve", self.rs_b, self.rs_b, "reciprocal", out=self.rs[:, 0:NMEM], in_=self.rs[:, 0:NMEM])
            for dt in range(DT):
                kb.op("dve", mfb + self.rs_b + [self.c_b], hmb, "scalar_tensor_tensor",
                      out=hm[:, dt, :], in0=mfv[:, dt, :], scalar=self.vcol(f"mem_norm{layer}", dt),
                      in1=self.rs[:, 0:NMEM], op0=ALU.mult, op1=ALU.mult)
        for dt in range(DT):
            s = self.wnext(f"wk{layer}", dt)
            if self.dry:
                continue
            pt, pb = self.ps()
            kb.group("pe", hmb + [self.ring_b[s]], [pb],
                     [("matmul", dict(out=pt[:, 0:NMEM], lhsT=self.wt(s, kt), rhs=hm[:, kt, :],
                                      start=(kt == 0), stop=(kt == DT - 1))) for kt in range(DT)])
            kb.op("act", [pb], [self.kv_b[seq][0]], "copy", out=self.ktb[:, seq, dt, :], in_=pt[:, 0:NMEM])
        for nb in range(2):
            sl = [self.wnext(f"wv{layer}", nb * 4 + q) for q in range(4)]
            if self.dry:
                continue
            for mt in range(2):
                pt, pb = self.ps()
                kb.group("pe", hmb + [self.ring_b[s] for s in sl], [pb],
                         [("matmul", dict(out=pt, lhsT=hm[:, kt, mt * 128:(mt + 1) * 128],
                                          rhs=self.ring[:, sl[kt // 2], (kt % 2) * 512:(kt % 2) * 512 + 512],
                                          start=(kt == 0), stop=(kt == DT - 1))) for kt in range(DT)])
                kb.op("act", [pb], [self.kv_b[seq][1]], "copy",
                      out=self.vvb[:, seq, mt, nb * 512:(nb + 1) * 512], in_=pt)

    def cross(self, layer, seq):
        kb = self.kb
        self.rmsnorm(f"cross_norm{layer}", lnexp=True)
        for dt in range(DT):
            s = self.wnext(f"wq{layer}", dt)
            if self.dry:
                continue
            for (c, c0, cn) in self.CH:
                pt, pb = self.ps()
                kb.group("pe", [self.h_b[kt][c] for kt in range(DT)] + [self.ring_b[s]], [pb],
                         [("matmul", dict(out=pt, lhsT=self.wt(s, kt), rhs=self.h[:, kt, c0:c0 + cn],
                                          start=(kt == 0), stop=(kt == DT - 1))) for kt in range(DT)])
                kb.op("act", [pb], [self.act_b[dt][c]], "copy", out=self.arow(dt, c * 512, 512), in_=pt)
        if not self.dry:
            for hd in range(4):
                for (c, c0, cn) in self.CH:
                    pr = 16 + (self.pbi % 4)
                    self.pbi += 1
                    pbufs = self.act_b[pr]
                    for mt in range(2):
                        pt, pb = self.ps()
                        kb.group("pe", [self.act_b[2 * hd][c], self.act_b[2 * hd + 1][c], self.kv_b[seq][0]], [pb],
                                 [("matmul", dict(out=pt, lhsT=self.ktb[:, seq, 2 * hd + a, mt * 128:(mt + 1) * 128],
                                                  rhs=self.arow(2 * hd + a, c * 512, 512),
                                                  start=(a == 0), stop=(a == 1))) for a in range(2)])
                        kb.op("act", [pb], [pbufs[mt]], "activation", out=self.arow(pr, mt * 512, 512), in_=pt,
                              func=AF.Exp, scale=1.0 / 16.0)
                    pt, pb = self.ps()
                    kb.group("pe", pbufs + [self.c_b], [pb],
                             [("matmul", dict(out=pt, lhsT=self.ones[:], rhs=self.arow(pr, mt * 512, 512),
                                              start=(mt == 0), stop=(mt == 1))) for mt in range(2)])
                    ri = self.rdi % 2
                    self.rdi += 1
                    kb.op("act", [pb], [self.rden_b[ri]], "activation", out=self.rden[:, ri, :], in_=pt, func=AF.Ln)
                    kb.op("act", [self.rden_b[ri]], [self.rden_b[ri]], "activation", out=self.rden[:, ri, :],
                          in_=self.rden[:, ri, :], func=AF.Exp, scale=-1.0)
                    for a in range(2):
                        dv = 2 * hd + a
                        pt, pb = self.ps()
                        kb.group("pe", pbufs + [self.kv_b[seq][1]], [pb],
                                 [("matmul", dict(out=pt, lhsT=self.vvb[:, seq, mt, dv * 128:(dv + 1) * 128],
                                                  rhs=self.arow(pr, mt * 512, 512),
                                                  start=(mt == 0), stop=(mt == 1))) for mt in range(2)])
                        kb.op("dve", [pb, self.rden_b[ri]], [self.act_b[8 + dv][c]], "tensor_tensor",
                              out=self.arow(8 + dv, c * 512, 512), in0=pt, in1=self.rden[:, ri, :], op=ALU.mult)
        for dt in range(DT):
            s = self.wnext(f"wo{layer}", dt)
            if self.dry:
                continue
            for (c, c0, cn) in self.CH:
                pt, pb = self.ps()
                kb.group("pe", [self.act_b[8 + kt][c] for kt in range(DT)] + [self.ring_b[s]], [pb],
                         [("matmul", dict(out=pt, lhsT=self.wt(s, kt), rhs=self.arow(8 + kt, c * 512, 512),
                                          start=(kt == 0), stop=(kt == DT - 1))) for kt in range(DT)])
                kb.op("dve", [pb, self.x_b[dt][c]], [self.x_b[dt][c]], "tensor_tensor",
                      out=self.x[:, dt, c0:c0 + cn], in0=pt, in1=self.x[:, dt, c0:c0 + cn], op=ALU.add)

    def load_x_halo(self, src, src_bufs, t0, seq0, seqlen):
        kb = self.kb
        allx = [b for r in self.x_b for b in r]
        srcv = src.ap().rearrange("(dt p) t -> p dt t", p=128)
        lo = max(t0 - HALO, seq0)
        hi = min(t0 + NT + HALO, seq0 + seqlen)
        if lo > t0 - HALO:
            kb.op("pool", [], allx, "memset", ap=self.x[:, :, 0:HALO], constant=0.0)
        if hi < t0 + NT + HALO:
            kb.op("pool", [], allx, "memset", ap=self.x[:, :, HALO + NT:XW], constant=0.0)
        kb.dma(src_bufs, allx, out=self.x[:, :, HALO + (lo - t0):HALO + (hi - t0)], in_=srcv[:, :, lo:hi])
        return lo > t0 - HALO, hi < t0 + NT + HALO

    def pool_mix(self, left_edge, right_edge):
        kb = self.kb
        wins = (2, 4, 8, 16)
        if not self.dry:
            allx = [b for r in self.x_b for b in r]
            self.calc_rstd([(0, 512), (512, 512), (1024, XW - 1024)])
            tsets = [[self.arow_f32(8 + 9 * q + 3 * k, 0, XW) for k in range(3)] for q in range(2)]
            for dt in range(DT):
                gi = dt // 2
                w = wins[gi]
                tmps = tsets[dt % 2]
                (xn, xnb) = tmps[0]
                kb.op("dve", allx + self.rs_b + [self.c_b], xnb, "scalar_tensor_tensor", out=xn, in0=self.x[:, dt, :],
                      scalar=self.vcol("mix_norm1", dt), in1=self.rs[:, 0:XW], op0=ALU.mult, op1=ALU.mult)
                en = "dve" if dt % 2 == 0 else "pool"
                cur, curb = tmps[1]
                kb.op(en, xnb, curb, "tensor_tensor", out=cur[:, 1:XW], in0=xn[:, 0:XW - 1], in1=xn[:, 1:XW], op=ALU.add)
                prev, prevb = cur, curb
                lo = 1
                for k, sh in ((2, 1), (3, 2), (4, 4)):
                    if wins[k - 1] > w:
                        break
                    cur, curb = tmps[1 + (k - 1) % 2]
                    lo2 = lo + sh
                    kb.op(en, prevb, curb, "tensor_tensor", out=cur[:, lo2:XW - lo2],
                          in0=prev[:, lo2 - sh:XW - lo2 - sh], in1=prev[:, lo2 + sh:XW - lo2 + sh], op=ALU.add)
                    prev, prevb, lo = cur, curb, lo2
                S = prev
                ob = self.act_b[dt]
                kb.op("dve", prevb + xnb, ob, "scalar_tensor_tensor", out=self.arow(dt), in0=S[:, HALO:HALO + NT],
                      scalar=1.0 / w, in1=xn[:, HALO:HALO + NT], op0=ALU.mult, op1=ALU.subtract)
                if left_edge:
                    for t in range(w // 2):
                        kb.op("dve", prevb + xnb, ob, "scalar_tensor_tensor", out=self.arow(dt, t, 1),
                              in0=S[:, HALO + t:HALO + t + 1], scalar=1.0 / (t + w // 2),
                              in1=xn[:, HALO + t:HALO + t + 1], op0=ALU.mult, op1=ALU.subtract)
                if right_edge:
                    for k in range(1, w // 2):
                        t = NT - k
                        kb.op("dve", prevb + xnb, ob, "scalar_tensor_tensor", out=self.arow(dt, t, 1),
                              in0=S[:, HALO + t:HALO + t + 1], scalar=1.0 / (k + w // 2),
                              in1=xn[:, HALO + t:HALO + t + 1], op0=ALU.mult, op1=ALU.subtract)
        for u in range(2):
            s = self.wnext("wpool", u)
            if self.dry:
                continue
            for gl in range(2):
                gi = 2 * u + gl
                for mt in range(2):
                    do = 2 * gi + mt
                    for (c, c0, cn) in self.CH:
                        pt, pb = self.ps()
                        kb.group("pe", [self.act_b[2 * gi][c], self.act_b[2 * gi + 1][c], self.ring_b[s]], [pb],
                                 [("matmul", dict(out=pt, lhsT=self.wt(s, gl * 4 + mt * 2 + kt),
                                                  rhs=self.arow(2 * gi + kt, c * 512, 512),
                                                  start=(kt == 0), stop=(kt == 1))) for kt in range(2)])
                        kb.op("dve", [pb, self.x_b[do][c], self.c_b], [self.x_b[do][c]], "scalar_tensor_tensor",
                              out=self.x[:, do, c0:c0 + cn], in0=pt, scalar=self.vcol("pool_scale", do),
                              in1=self.x[:, do, c0:c0 + cn], op0=ALU.mult, op1=ALU.add)


def build_program(NU, table, stage=STAGE):
    dry = Prog(NU, table, None, stage)
    dry.build_dry()
    real = Prog(NU, table, dry.sched, stage)
    return real.build(), real.order


def weight_table():
    table, base = {}, 0
    for i in range(2):
        for f in ("ffn1", "ffn2"):
            for nm, n in ((f"{f}_gate{i}", 32), (f"{f}_up{i}", 32), (f"{f}_down{i}", 32)):
                table[nm] = base
                base += n
        for nm, n in ((f"wq{i}", 8), (f"wk{i}", 8), (f"wv{i}", 8), (f"wo{i}", 8)):
            table[nm] = base
            base += n
    for nm, n in (("win_a", 8), ("win_v", 4), ("wglu", 2), ("wout", 8), ("wpool", 2)):
        table[nm] = base
        base += n
    return base, table


def make_inputs(inp, order=None):
    wall, table = pack_weights(inp)
    if order is not None:
        wall = np.ascontiguousarray(wall[np.asarray(order)].transpose(1, 0, 2)).reshape(128, -1)
    vecs = pack_vecs(inp)
    s5p = pack_s5(inp)
    sgup = pack_sgu(inp)
    in_maps = []
    for c in range(8):
        xT = np.ascontiguousarray(
            np.concatenate([np.asarray(inp["x_prompt"][c]), np.asarray(inp["x_sample"][c])], 0).T)
        memT = np.ascontiguousarray(
            np.concatenate([np.asarray(inp["mem_prompt"][c]), np.asarray(inp["mem_sample"][c])], 0).T)
        in_maps.append({"xT": xT, "memT": memT, "wall": wall, "vecs": vecs, "s5p": s5p, "sgup": sgup})
    assert table == weight_table()[1]
    return in_maps, weight_table()[0], table


def kernel(**inp):
    inp = {k: np.asarray(v) for k, v in inp.items()}
    NU, table = weight_table()
    nc, order = build_program(NU, table)
    in_maps, _, _ = make_inputs(inp, order)
    res = run_bass_kernel_spmd(nc, in_maps, core_ids=list(range(8)))
    yp = np.stack([np.ascontiguousarray(res.results[c]["yT"][:, :LP].T) for c in range(8)], 0)
    ys = np.stack([np.ascontiguousarray(res.results[c]["yT"][:, LP:].T) for c in range(8)], 0)
    return (yp.astype(np.float32), ys.astype(np.float32))
```
